# Optimizing a Trainium2 kernel written in Bass

```python
import math
import jax, jax.numpy as jnp
from jax import lax
import numpy as np

D_MODEL = 1024
BATCH = 8
SEQ = 8192
DEPTH = 1

SSD_EXPAND = 2
D_INNER = SSD_EXPAND * D_MODEL
SSD_HEAD_DIM = 64
SSD_HEADS = D_INNER // SSD_HEAD_DIM
SSD_GROUPS = 4
SSD_HEADS_PER_GROUP = SSD_HEADS // SSD_GROUPS
D_STATE = 128
D_CONV = 5
SSD_CHUNK = 128
CONV_DIM = D_INNER + 2 * SSD_GROUPS * D_STATE
NORM_EPS = 1e-5

ATTN_HEAD_DIM = 64
DIL_PATTERNS = ((128, 1), (512, 4), (2048, 16))
N_PATTERNS = len(DIL_PATTERNS)
HEADS_PER_PATTERN = 4
ATTN_HEADS = N_PATTERNS * HEADS_PER_PATTERN
ATTN_WIDTH = ATTN_HEADS * ATTN_HEAD_DIM
ATTN_OUT_WIDTH = HEADS_PER_PATTERN * ATTN_HEAD_DIM

D_FF = 4 * D_MODEL
N_BRANCHES = 2
IN_SPLITS = (D_INNER, CONV_DIM, SSD_HEADS, SSD_HEADS, ATTN_WIDTH, ATTN_WIDTH, ATTN_WIDTH, N_BRANCHES * D_MODEL)
IN_COLS = sum(IN_SPLITS)

kernel_name = "hybrid_ssd_dilated_attn_gated_deepnorm"


def layer_norm(x, g, b):
    xf = x.astype(jnp.float32)
    mu = jnp.mean(xf, axis=-1, keepdims=True)
    var = jnp.mean(jnp.square(xf - mu), axis=-1, keepdims=True)
    return ((xf - mu) * lax.rsqrt(var + NORM_EPS) * g.astype(jnp.float32) + b.astype(jnp.float32)).astype(x.dtype)


def centred_depthwise_conv(u, w, b):
    pad = D_CONV // 2
    out = lax.conv_general_dilated(u, w[:, None, :].astype(u.dtype), window_strides=(1,), padding=((pad, pad),),
                                   dimension_numbers=('NWC', 'WIO', 'NWC'), feature_group_count=u.shape[-1])
    return out + b.astype(u.dtype)


def segsum_exp(a_cs):
    q = a_cs.shape[-1]
    diff = a_cs[..., :, None] - a_cs[..., None, :]
    mask = jnp.tril(jnp.ones((q, q), dtype=bool))
    return jnp.where(mask, jnp.exp(jnp.where(mask, diff, 0.0)), 0.0)


def ssd_chunked(xh, dt, a_coef, bm, cm):
    bsz, s = xh.shape[:2]
    nc = s // SSD_CHUNK
    g, r, p = SSD_GROUPS, SSD_HEADS_PER_GROUP, SSD_HEAD_DIM
    xc = (xh * dt[..., None]).reshape(bsz, nc, SSD_CHUNK, g, r, p)
    a = (dt * a_coef).reshape(bsz, nc, SSD_CHUNK, g, r).transpose(0, 3, 4, 1, 2)
    bc = bm.reshape(bsz, nc, SSD_CHUNK, g, D_STATE)
    cc = cm.reshape(bsz, nc, SSD_CHUNK, g, D_STATE)
    a_cs = jnp.cumsum(a, axis=-1)
    lmat = segsum_exp(a_cs)
    cb = jnp.einsum('bclgn,bcsgn->bgcls', cc, bc)
    y_diag = jnp.einsum('bgcls,bgrcls,bcsgrp->bclgrp', cb, lmat, xc)
    decay_states = jnp.exp(a_cs[..., -1:] - a_cs)
    states = jnp.einsum('bclgn,bgrcl,bclgrp->bcgrpn', bc, decay_states, xc)
    chunk_decay = jnp.exp(a_cs[..., -1])

    def step(h, inp):
        dec, st = inp
        return dec[..., None, None] * h + st, h

    h0 = jnp.zeros_like(states[:, 0])
    _, prev = lax.scan(step, h0, (jnp.moveaxis(chunk_decay, -1, 0), jnp.moveaxis(states, 1, 0)))
    prev = jnp.moveaxis(prev, 0, 1)
    y_off = jnp.einsum('bclgn,bcgrpn,bgrcl->bclgrp', cc, prev, jnp.exp(a_cs))
    return (y_diag + y_off).reshape(bsz, s, g * r, p)


def ssd_branch(z, xbc, dt_f_raw, dt_b_raw, conv_w, conv_b, dt_bias_f, dt_bias_b, a_log_f, a_log_b, d_skip, ssd_norm_w):
    bsz, s = z.shape[:2]
    xbc = jax.nn.silu(centred_depthwise_conv(xbc, conv_w, conv_b)).astype(jnp.float32)
    xs, bm, cm = jnp.split(xbc, [D_INNER, D_INNER + SSD_GROUPS * D_STATE], axis=-1)
    xh = xs.reshape(bsz, s, SSD_HEADS, SSD_HEAD_DIM)
    bm = bm.reshape(bsz, s, SSD_GROUPS, D_STATE)
    cm = cm.reshape(bsz, s, SSD_GROUPS, D_STATE)
    dt_f = jax.nn.softplus(dt_f_raw.astype(jnp.float32) + dt_bias_f.astype(jnp.float32))
    dt_b = jax.nn.softplus(dt_b_raw.astype(jnp.float32) + dt_bias_b.astype(jnp.float32))
    a_f = -jnp.exp(a_log_f.astype(jnp.float32))
    a_b = -jnp.exp(a_log_b.astype(jnp.float32))
    y_f = ssd_chunked(xh, dt_f, a_f, bm, cm)
    flip = lambda t: jnp.flip(t, axis=1)
    y_b = flip(ssd_chunked(flip(xh), flip(dt_b), a_b, flip(bm), flip(cm)))
    y = y_f + y_b + d_skip.astype(jnp.float32)[:, None] * xh
    y = y.reshape(bsz, s, D_INNER) * jax.nn.silu(z.astype(jnp.float32))
    yg = y.reshape(bsz, s, SSD_GROUPS, D_INNER // SSD_GROUPS)
    yg = yg * lax.rsqrt(jnp.mean(jnp.square(yg), axis=-1, keepdims=True) + NORM_EPS)
    return (yg.reshape(bsz, s, D_INNER) * ssd_norm_w.astype(jnp.float32)).astype(z.dtype)


def dilated_window_attention(q, k, v, slopes, dilation, half):
    bsz, s, h, e = q.shape
    seq_l = s // dilation
    blk = half
    nb = -(-seq_l // blk)
    lp = nb * blk

    def to_strided(a):
        return a.astype(jnp.float32).reshape(bsz, seq_l, dilation, h, e).transpose(0, 2, 3, 1, 4)

    qs = jnp.pad(to_strided(q), ((0, 0), (0, 0), (0, 0), (0, lp - seq_l), (0, 0))).reshape(bsz, dilation, h, nb, blk, e)

    def windows(a):
        a = jnp.pad(to_strided(a), ((0, 0), (0, 0), (0, 0), (blk, blk + lp - seq_l), (0, 0)))
        a = a.reshape(bsz, dilation, h, nb + 2, blk, e)
        return jnp.concatenate([a[:, :, :, :-2], a[:, :, :, 1:-1], a[:, :, :, 2:]], axis=4)

    ks, vs = windows(k), windows(v)
    qpos = jnp.arange(nb)[:, None] * blk + jnp.arange(blk)[None, :]
    kpos = jnp.arange(nb)[:, None] * blk - blk + jnp.arange(3 * blk)[None, :]
    rel = kpos[:, None, :] - qpos[:, :, None]
    valid = (jnp.abs(rel) <= half) & (kpos[:, None, :] >= 0) & (kpos[:, None, :] < seq_l)
    dist = (jnp.abs(rel) * dilation).astype(jnp.float32)
    scores = jnp.einsum('bdhine,bdhime->bdhinm', qs, ks) * (1.0 / math.sqrt(e))
    scores = scores - slopes.astype(jnp.float32)[:, None, None, None] * dist
    scores = jnp.where(valid, scores, -jnp.inf)
    m = jnp.max(scores, axis=-1, keepdims=True)
    p = jnp.exp(scores - m)
    den = jnp.sum(p, axis=-1, keepdims=True)
    o = jnp.einsum('bdhinm,bdhime->bdhine', p, vs) / den
    lse = (m + jnp.log(den))[..., 0]

    def from_strided(a):
        a = a.reshape(bsz, dilation, h, lp, *a.shape[5:])[:, :, :, :seq_l]
        a = jnp.moveaxis(a, 3, 1)
        return a.reshape(bsz, s, h, *a.shape[4:])

    return from_strided(o), from_strided(lse)


def attention_branch(q, k, v):
    bsz, s = q.shape[:2]
    q = q.reshape(bsz, s, ATTN_HEADS, ATTN_HEAD_DIM)
    k = k.reshape(bsz, s, ATTN_HEADS, ATTN_HEAD_DIM)
    v = v.reshape(bsz, s, ATTN_HEADS, ATTN_HEAD_DIM)
    slopes = jnp.asarray(2.0 ** (-8.0 * np.arange(1, ATTN_HEADS + 1) / ATTN_HEADS), dtype=jnp.float32)
    outs, lses = [], []
    for gi, (window, dilation) in enumerate(DIL_PATTERNS):
        hs = slice(gi * HEADS_PER_PATTERN, (gi + 1) * HEADS_PER_PATTERN)
        o, l = dilated_window_attention(q[:, :, hs], k[:, :, hs], v[:, :, hs], slopes[hs], dilation, window // (2 * dilation))
        outs.append(o)
        lses.append(l)
    o = jnp.stack(outs, axis=0)
    lse = jnp.stack(lses, axis=0)
    w = jax.nn.softmax(lse, axis=0)
    y = jnp.sum(w[..., None] * o, axis=0)
    return y.reshape(bsz, s, ATTN_OUT_WIDTH).astype(q.dtype)


def setup_inputs(seed: int = 0) -> dict:
    key = jax.random.key(seed)
    ks = jax.random.split(key, 24)
    beta = (8.0 * DEPTH) ** -0.25
    nrm = lambda k, shape: jax.random.normal(k, shape, dtype=jnp.float32)

    def dt_bias(k):
        dt = jnp.exp(jax.random.uniform(k, (SSD_HEADS,), minval=math.log(1e-3), maxval=math.log(1e-1)))
        return dt + jnp.log(-jnp.expm1(-dt))

    return {
        "x": nrm(ks[0], (BATCH, SEQ, D_MODEL)),
        "w_in": nrm(ks[1], (D_MODEL, IN_COLS)) * D_MODEL ** -0.5,
        "b_gate": 0.02 * nrm(ks[2], (N_BRANCHES * D_MODEL,)),
        "conv_w": nrm(ks[3], (D_CONV, CONV_DIM)) * D_CONV ** -0.5,
        "conv_b": 0.02 * nrm(ks[4], (CONV_DIM,)),
        "dt_bias_f": dt_bias(ks[5]),
        "dt_bias_b": dt_bias(ks[6]),
        "a_log_f": jnp.log(jax.random.uniform(ks[7], (SSD_HEADS,), minval=1.0, maxval=16.0)),
        "a_log_b": jnp.log(jax.random.uniform(ks[8], (SSD_HEADS,), minval=1.0, maxval=16.0)),
        "d_skip": 1.0 + 0.1 * nrm(ks[9], (SSD_HEADS,)),
        "ssd_norm_w": 1.0 + 0.1 * nrm(ks[10], (D_INNER,)),
        "w_proj_ssd": nrm(ks[11], (D_INNER, D_MODEL)) * D_INNER ** -0.5 * beta,
        "w_proj_attn": nrm(ks[12], (ATTN_OUT_WIDTH, D_MODEL)) * ATTN_OUT_WIDTH ** -0.5 * beta,
        "w_out": nrm(ks[13], (D_MODEL, D_MODEL)) * D_MODEL ** -0.5 * beta,
        "ln1_g": 1.0 + 0.1 * nrm(ks[14], (D_MODEL,)),
        "ln1_b": 0.02 * nrm(ks[15], (D_MODEL,)),
        "w_up": nrm(ks[16], (D_MODEL, D_FF)) * D_MODEL ** -0.5 * beta,
        "w_down": nrm(ks[17], (D_FF, D_MODEL)) * D_FF ** -0.5 * beta,
        "ln2_g": 1.0 + 0.1 * nrm(ks[18], (D_MODEL,)),
        "ln2_b": 0.02 * nrm(ks[19], (D_MODEL,)),
    }


def reference(x, w_in, b_gate, conv_w, conv_b, dt_bias_f, dt_bias_b, a_log_f, a_log_b, d_skip, ssd_norm_w,
              w_proj_ssd, w_proj_attn, w_out, ln1_g, ln1_b, w_up, w_down, ln2_g, ln2_b):
    alpha = (2.0 * DEPTH) ** 0.25
    h = x
    for _ in range(DEPTH):
        bsz, s = h.shape[:2]
        u = h @ w_in
        z, xbc, dt_f_raw, dt_b_raw, q, k, v, gate_logits = jnp.split(u, list(np.cumsum(IN_SPLITS)[:-1]), axis=-1)
        y_ssd = ssd_branch(z, xbc, dt_f_raw, dt_b_raw, conv_w, conv_b, dt_bias_f, dt_bias_b,
                           a_log_f, a_log_b, d_skip, ssd_norm_w) @ w_proj_ssd
        y_att = attention_branch(q, k, v) @ w_proj_attn
        gates = jax.nn.sigmoid(gate_logits + b_gate).reshape(bsz, s, N_BRANCHES, D_MODEL)
        mix = (gates[:, :, 0] * y_ssd + gates[:, :, 1] * y_att) @ w_out
        h = layer_norm(alpha * h + mix, ln1_g, ln1_b)
        f = jnp.square(jax.nn.relu(h @ w_up)) @ w_down
        h = layer_norm(alpha * h + f, ln2_g, ln2_b)
    return h
```

```python
import contextlib
import math
import os
import numpy as np
import concourse.bass as bass
import concourse.mybir as mybir
from concourse.bass_utils import run_bass_kernel_spmd

F32 = mybir.dt.float32
BF16 = mybir.dt.bfloat16
ALU = mybir.AluOpType
AF = mybir.ActivationFunctionType
AX = mybir.AxisListType

ENGS = ['sp', 'act', 'dve', 'pool', 'pe']
STRICT_ENG = ('pool', 'dve')

D = 1024
SEQ = 8192
NCORES = 8
DIN = 2048
NH = 32
HP = 64
NG = 4
DS = 128
CONV = 3072
AW = 768
DFF = 4096
C_Z, C_X, C_DT, C_Q, C_K, C_V, C_G = 0, 2048, 5120, 5184, 5952, 6720, 7488
INC = 9536
EPS = 1e-5
ALPHA = 2.0 ** 0.25
NEG = -30000.0


class _Rec:
    def __init__(self):
        self.name = None
        self.kw = {}
        self.args = ()

    def __getattr__(self, name):
        def f(*args, **kw):
            self.name = name
            self.kw = kw
            self.args = args
            return self
        return f

    def then_inc(self, *a, **k):
        return self


def _free_size(ap):
    try:
        shp = ap.shape
        n = 1
        for d_ in shp[1:]:
            n *= int(d_)
        return n
    except Exception:
        return 256


NDMASEM = 14
SAME_CLASS_RELAX = int(os.environ.get('SAME_CLASS_RELAX', '0'))
ACT_SPLIT = int(os.environ.get('ACT_SPLIT', '1'))
SYNC_NS = float(os.environ.get('G_SYNC', '2000'))
import os
WINDOW = int(os.environ.get('G_WINDOW', '96'))


class G:
    def __init__(self, nc, es):
        self.nc = nc
        self.es = es
        self.sems = {}
        self.cnt = {}
        self.waited = {e: {} for e in ENGS}
        self.nodes = []
        self.lastw = {}
        self.readers = {}
        self.nins = 0
        self.drain_eng = None
        self.dma_i = {}

    def sem(self, key):
        if key not in self.sems:
            self.sems[key] = self.es.enter_context(self.nc.semaphore(key))
            self.cnt[key] = 0
        return self.sems[key]

    def _preds(self, reads, writes):
        ps = set()
        for r in reads:
            w = self.lastw.get(r)
            if w is not None:
                ps.add(w)
        for w_ in writes:
            w = self.lastw.get(w_)
            if w is not None:
                ps.add(w)
            for d in self.readers.get(w_, ()):
                ps.add(d)
        return ps

    def _mark(self, me, reads, writes):
        for w in writes:
            self.lastw[w] = me
            self.readers[w] = []
        for r in reads:
            self.readers.setdefault(r, []).append(me)

    def _cost(self, eng, fn):
        rec = _Rec()
        try:
            fn(rec)
        except Exception:
            return 300.0
        kw = rec.kw
        out = kw.get('out', rec.args[0] if rec.args else None)
        n = _free_size(out) if out is not None else 256
        if eng == 'pe':
            if rec.name == 'transpose':
                return 75.0
            rhs = kw.get('rhs')
            nn = _free_size(rhs) if rhs is not None else n
            passes = 4.0 if (rhs is not None and rhs.dtype == F32) else 1.0
            return max(nn, 64) * passes / 2.4 + 60.0
        if eng == 'act':
            extra = 0.0
            for k_ in ('bias', 'scale', 'accum_out'):
                v = kw.get(k_)
                if v is not None and not isinstance(v, (int, float)):
                    extra += 93.0
            return 160.0 + n / 1.4 + extra
        if eng == 'dve':
            return 70.0 + n / 0.96
        if eng == 'pool':
            return 120.0 + n * 1.6
        return 100.0

    def _semkey(self, eng, fn):
        if eng not in ('act', 'dve') or not ACT_SPLIT:
            return eng
        rec = _Rec()
        try:
            fn(rec)
            srcs = [rec.kw.get(k_) for k_ in ('in_', 'in0', 'in1', 'data', 'on_true', 'on_false')]
            if len(rec.args) > 1:
                srcs.append(rec.args[1])
            for src in srcs:
                if src is not None and hasattr(src, 'space') and 'PSUM' in str(src.space).upper():
                    return eng + '_p'
            return eng
        except Exception:
            return eng

    def op(self, eng, fn, reads=(), writes=()):
        preds = self._preds(reads, writes)
        i = len(self.nodes)
        self.nodes.append(dict(eng=eng, fn=fn, preds=preds, key=self._semkey(eng, fn), inc=1, cost=self._cost(eng, fn), lat=0.0))
        self._mark(i, reads, writes)
        self.nins += 1

    def dma(self, queue, stream, out, in_, reads=(), writes=(), **kw):
        preds = self._preds(reads, writes)
        i = len(self.nodes)
        nbytes = 0
        try:
            nbytes = out.nbytes()
        except Exception:
            nbytes = 65536
        self.nodes.append(dict(eng=queue, fn=(lambda e: e.dma_start(out=out, in_=in_, **kw)), preds=preds, key='d_' + stream, inc=16,
                               cost=60.0, lat=2000.0 + nbytes / 60.0))
        self._mark(i, reads, writes)
        self.nins += 1

    def drain(self, eng='sp'):
        self.drain_eng = eng

    def _schedule(self):
        nodes = self.nodes
        n = len(nodes)
        chain = [c_ for c_ in os.environ.get('G_CHAIN', '').split(',') if c_]
        last = {}
        for i, nd in enumerate(nodes):
            if nd['eng'] in chain:
                if nd['eng'] in last:
                    nd['preds'] = set(nd['preds']) | {last[nd['eng']]}
                last[nd['eng']] = i
        succ = [[] for _ in range(n)]
        indeg = [0] * n
        for i, nd in enumerate(nodes):
            nd['preds'] = [p for p in nd['preds'] if p < n]
            indeg[i] = len(nd['preds'])
            for p in nd['preds']:
                succ[p].append(i)
        import heapq
        ready = [i for i in range(n) if indeg[i] == 0]
        heapq.heapify(ready)
        finish = [0.0] * n
        eng_free = {e: 0.0 for e in ENGS}
        last_on = {}
        order = []
        while ready:
            cand = heapq.nsmallest(WINDOW, ready)
            best, best_t = None, None
            for i in cand:
                nd = nodes[i]
                t = eng_free[nd['eng']]
                for p in nd['preds']:
                    tp = finish[p] + (SYNC_NS if nodes[p]['eng'] != nd['eng'] or nodes[p]['inc'] == 16 else 0.0)
                    if tp > t:
                        t = tp
                if best is None or t < best_t - 1e-9:
                    best, best_t = i, t
            ready.remove(best)
            heapq.heapify(ready)
            nd = nodes[best]
            why = ('eng', last_on.get(nd['eng']))
            tb = eng_free[nd['eng']]
            for p in nd['preds']:
                tp = finish[p] + (SYNC_NS if nodes[p]['eng'] != nd['eng'] or nodes[p]['inc'] == 16 else 0.0)
                if tp > tb:
                    tb = tp
                    why = ('dep', p)
            nd['why'] = why
            nd['t0'] = best_t
            last_on[nd['eng']] = best
            eng_free[nd['eng']] = best_t + nd['cost']
            finish[best] = best_t + nd['cost'] + nd['lat']
            order.append(best)
            for s_ in succ[best]:
                indeg[s_] -= 1
                if indeg[s_] == 0:
                    heapq.heappush(ready, s_)
        assert len(order) == n
        self.sim_ns = max(finish) if n else 0.0
        busy = {e: 0.0 for e in ENGS}
        for nd in nodes:
            busy[nd['eng']] += nd['cost']
        self.busy = busy
        if os.environ.get('G_CRIT') and n:
            i = max(range(n), key=lambda j: finish[j])
            chain = []
            while i is not None and len(chain) < 400:
                nd = nodes[i]
                chain.append((i, nd['eng'], nd['why'][0], round(nd['t0'] / 1000.0, 1), round(nd['cost'])))
                i = nd['why'][1]
            tot = {}
            for (_, e_, w_, _, c_) in chain:
                tot[(e_, w_)] = tot.get((e_, w_), 0) + c_
            print("CRIT tail:", chain[:60])
            print("CRIT totals(ns) over last %d nodes:" % len(chain), tot)
        return order

    def flush(self):
        nc = self.nc
        nodes = self.nodes
        order = self._schedule()
        q = {e: [] for e in ENGS}
        semval = {}
        for i in order:
            nd = nodes[i]
            eng = nd['eng']
            deps = {}
            for p in nd['preds']:
                pk, pv = semval[p]
                pn = nodes[p]
                if pn['eng'] == eng and pn['inc'] == 1 and (eng == 'pe' or eng not in STRICT_ENG or (SAME_CLASS_RELAX and eng == 'dve' and pn['key'] == nd['key'])):
                    continue
                if deps.get(pk, 0) < pv:
                    deps[pk] = pv
            waits = []
            for k, v in deps.items():
                if self.waited[eng].get(k, 0) >= v:
                    continue
                self.waited[eng][k] = v
                waits.append((k, v))
            key = nd['key']
            if nd['inc'] == 16:
                self.dma_i[eng] = self.dma_i.get(eng, 0) + 1
                key = 'dq_%s_%d' % (eng, self.dma_i[eng] % NDMASEM)
                self.sem(key)
                if self.cnt[key] > 0 and self.waited[eng].get(key, 0) < self.cnt[key]:
                    self.waited[eng][key] = self.cnt[key]
                    waits.append((key, self.cnt[key]))
            self.sem(key)
            self.cnt[key] += nd['inc']
            semval[i] = (key, self.cnt[key])
            q[eng].append((waits, nd['fn'], key, nd['inc']))
        if self.drain_eng is not None:
            eng = self.drain_eng
            waits = []
            for k, v in self.cnt.items():
                if v > 0 and self.waited[eng].get(k, 0) < v and k != eng:
                    self.waited[eng][k] = v
                    waits.append((k, v))
            q[eng].append((waits, None, None, 0))
            self.drain_eng = None
        sems = self.sems

        def emit(e, lst):
            for waits, fn, key, inc in lst:
                for k, v in waits:
                    e.wait_ge(sems[k], v)
                if fn is not None:
                    fn(e).then_inc(sems[key], inc)

        with nc.Block() as block:
            @block.sync
            def _(e):
                emit(e, q['sp'])

            @block.scalar
            def _(e):
                emit(e, q['act'])

            @block.vector
            def _(e):
                emit(e, q['dve'])

            @block.gpsimd
            def _(e):
                emit(e, q['pool'])

            @block.tensor
            def _(e):
                emit(e, q['pe'])
        print("phase: %d nodes, sim %.0f us, busy us: %s" % (len(nodes), self.sim_ns / 1000.0,
                                                              ' '.join('%s=%.0f' % (e, v / 1000.0) for e, v in getattr(self, 'busy', {}).items())))
        self.nodes = []
        self.lastw = {}
        self.readers = {}


class RR:
    def __init__(self, engs):
        self.engs = engs
        self.i = 0

    def __call__(self):
        e = self.engs[self.i % len(self.engs)]
        self.i += 1
        return e


def copy_op(g, eng, out, in_, reads, writes):
    if eng == 'act':
        g.op('act', lambda e: e.copy(out=out, in_=in_), reads=reads, writes=writes)
    else:
        g.op(eng, lambda e: e.tensor_copy(out=out, in_=in_), reads=reads, writes=writes)


def make_consts(nc, g, es):
    c = {}
    sb = lambda name, shape, dt: es.enter_context(nc.sbuf_tensor(name, shape, dt))
    c['identf'] = sb("identf", [128, 128], F32)
    c['ident'] = sb("ident", [128, 128], BF16)
    c['Uf'] = sb("Uf", [128, 128], F32)
    c['Ub'] = sb("Ub", [128, 128], F32)
    c['onesf'] = sb("onesf", [128, 128], F32)
    c['onesb'] = sb("onesb", [128, 128], BF16)
    c['maskf'] = sb("maskf", [128, 512], BF16)
    c['maskb'] = sb("maskb", [128, 512], BF16)
    identf, ident, Uf, Ub, onesf, onesb, maskf, maskb = (c[k] for k in
                                                          ['identf', 'ident', 'Uf', 'Ub', 'onesf', 'onesb', 'maskf', 'maskb'])
    P = 'pool'
    g.op(P, lambda e: e.memset(onesf[:, :], 1.0), writes=['onesf'])
    g.op(P, lambda e: e.memset(onesb[:, :], 1.0), writes=['onesb'])
    g.op(P, lambda e: e.memset(identf[:, :], 1.0), writes=['identf'])
    g.op(P, lambda e: e.affine_select(out=identf[:, :], in_=identf[:, :], pattern=[[1, 128]], compare_op=ALU.is_equal,
                                      fill=0.0, base=0, channel_multiplier=-1), reads=['identf'], writes=['identf'])
    g.op(P, lambda e: e.tensor_copy(out=ident[:, :], in_=identf[:, :]), reads=['identf'], writes=['ident'])
    g.op(P, lambda e: e.memset(Uf[:, :], 1.0), writes=['Uf'])
    g.op(P, lambda e: e.affine_select(out=Uf[:, :], in_=Uf[:, :], pattern=[[1, 128]], compare_op=ALU.is_ge,
                                      fill=0.0, base=0, channel_multiplier=-1), reads=['Uf'], writes=['Uf'])
    g.op(P, lambda e: e.memset(Ub[:, :], 1.0), writes=['Ub'])
    g.op(P, lambda e: e.affine_select(out=Ub[:, :], in_=Ub[:, :], pattern=[[-1, 128]], compare_op=ALU.is_ge,
                                      fill=0.0, base=0, channel_multiplier=1), reads=['Ub'], writes=['Ub'])
    g.op(P, lambda e: e.memset(maskf[:, :], 0.0), writes=['maskf'])
    g.op(P, lambda e: e.memset(maskb[:, :], 0.0), writes=['maskb'])
    for h in range(4):
        sl = slice(h * 128, (h + 1) * 128)
        g.op(P, lambda e, sl=sl: e.affine_select(out=maskf[:, sl], in_=maskf[:, sl], pattern=[[1, 128]], compare_op=ALU.is_ge,
                                                 fill=NEG, base=0, channel_multiplier=-1), reads=['maskf'], writes=['maskf'])
        g.op(P, lambda e, sl=sl: e.affine_select(out=maskb[:, sl], in_=maskb[:, sl], pattern=[[-1, 128]], compare_op=ALU.is_ge,
                                                 fill=NEG, base=0, channel_multiplier=1), reads=['maskb'], writes=['maskb'])
    return c


WCH = {}


def load_weight_bf16(g, nc, es, W, wdram, r0, c0, c1, name, nk, stage, stage_keys, rr, chunk=1024):
    WCH[name] = chunk
    si = 0
    for cc in range(c0, c1, chunk):
        for kc in range(nk):
            ce = min(c1, cc + chunk)
            n = ce - cc
            st, sk = stage[si % len(stage)], stage_keys[si % len(stage)]
            si += 1
            g.dma(['sp', 'pool'][si % 2], 'wst' + sk, st[:, 0:n], wdram[r0 + kc * 128:r0 + (kc + 1) * 128, cc:ce], writes=[sk])
            copy_op(g, rr(), W[:, kc, cc - c0:ce - c0], st[:, 0:n], reads=[sk], writes=[(name, kc, (cc - c0) // chunk)])


def wk(name, kc, lo, hi):
    ch = WCH[name]
    return [(name, kc, b_) for b_ in range(lo // ch, (hi - 1) // ch + 1)]


def x_tile_T(g, nc, c, xdram, t0, nchunk, xin, xb, xT, ptr, keys, evac_eng):
    kin, kb, kT, kp = keys
    g.dma('sp', kin, xin[:, 0:nchunk, :], xdram[t0:t0 + 128 * nchunk, :].rearrange("(j p) d -> p j d", p=128), writes=[kin])
    g.op('act', lambda e: e.copy(out=xb[:, 0:nchunk, :], in_=xin[:, 0:nchunk, :]), reads=[kin], writes=[kb])
    for j in range(nchunk):
        for kc in range(8):
            g.op('pe', lambda e, j=j, kc=kc: e.transpose(out=ptr[:, kc, :], in_=xb[:, j, kc * 128:(kc + 1) * 128],
                                                       identity=c['ident'][:, :]), reads=[kb, 'ident'], writes=[kp])
        copy_op(g, evac_eng, xT[:, :, j * 128:(j + 1) * 128], ptr[:, :, :], reads=[kp], writes=[kT])


def phase_A1(nc, g, c, S, io, scr):
    T = 256
    NT = S // T
    with contextlib.ExitStack() as es:
        sb = lambda name, shape, dt: es.enter_context(nc.sbuf_tensor(name, shape, dt))
        ps = lambda name, shape, dt: es.enter_context(nc.psum_tensor(name, shape, dt))
        Wa = sb("Wa", [128, 8, 3136], BF16)
        wst = [sb("wst%d" % i, [128, 512], F32) for i in range(4)]
        cw6 = sb("cw6", [6, CONV], F32)
        wcol = sb("wcol", [128, 24, 8], F32)
        diag = sb("diag", [128, 24, 5, 128], BF16)
        dtb = sb("dtb", [128, 64], F32)
        Acoef = sb("Acoef", [128, 64], F32)
        uT = [sb("uT%d" % i, [128, 24, T + 4], BF16) for i in range(2)]
        xin = sb("xin", [128, 2, 1024], F32)
        xb = sb("xb", [128, 2, 1024], BF16)
        xT = [sb("xT%d" % i, [128, 8, T], BF16) for i in range(2)]
        xcst = [sb("xcst%d" % i, [128, 2560], BF16) for i in range(2)]
        bcst = [sb("bcst%d" % i, [128, 8, T], BF16) for i in range(2)]
        dtraw = sb("dtraw", [128, S // 128, 64], F32)
        dtw1 = sb("dtw1", [128, S // 512, 64], F32)
        dtout = sb("dtout", [128, S // 512, 128], F32)
        pin = [ps("pin%d" % i, [128, 512], F32) for i in range(2)]
        ptr = ps("ptr", [128, 8, 128], BF16)
        pdt = ps("pdt", [128, 512], F32)
        pcv = [ps("pcv%d" % i, [128, 512], F32) for i in range(2)]
        ptrc = [ps("ptrc%d" % i, [128, 8, 128], BF16) for i in range(2)]
        pmisc = pcv[0][:, 0:192].rearrange("p (a b) -> p a b", b=8)
        xfm = sb("xfm", [128, 16, T], BF16)

        rr = RR(['dve', 'act'])
        load_weight_bf16(g, nc, es, Wa, io['w_in'], 0, C_X, C_X + 3136, 'Wa', 8, wst, ['wst%d' % i for i in range(4)], rr, chunk=512)
        g.dma('pool', 'cw', cw6[1:6, :], io['conv_w'][:, :], writes=['cw6'])
        g.dma('pool', 'cw', cw6[0:1, :], io['conv_b'].rearrange("(o n) -> o n", o=1), writes=['cw6'])
        for ct in range(24):
            g.op('pe', lambda e, ct=ct: e.matmul(pmisc[:, ct, 0:6], lhsT=cw6[0:6, ct * 128:(ct + 1) * 128], rhs=c['identf'][0:6, 0:6],
                                                 start=True, stop=True), reads=['cw6', 'identf'], writes=['pcv0'])
        g.op('dve', lambda e: e.tensor_copy(out=wcol[:, :, :], in_=pmisc[:, :, :]), reads=['pcv0'], writes=['wcol'])
        for ct in range(24):
            for k in range(5):
                g.op(['dve', 'pool'][(ct * 5 + k) % 2],
                     lambda e, ct=ct, k=k: e.tensor_scalar_mul(out=diag[:, ct, k, :], in0=c['identf'][:, :], scalar1=wcol[:, ct, k + 1:k + 2]),
                     reads=['wcol', 'identf'], writes=['diag'])
        g.dma('pool', 'cw', dtb[:, 0:32], io['dt_bias_f'].rearrange("(o n) -> o n", o=1).broadcast_to([128, 32]), writes=['dtb'])
        g.dma('pool', 'cw', dtb[:, 32:64], io['dt_bias_b'].rearrange("(o n) -> o n", o=1).broadcast_to([128, 32]), writes=['dtb'])
        g.dma('pool', 'cw', Acoef[:, 0:32], io['a_log_f'].rearrange("(o n) -> o n", o=1).broadcast_to([128, 32]), writes=['Acoef'])
        g.dma('pool', 'cw', Acoef[:, 32:64], io['a_log_b'].rearrange("(o n) -> o n", o=1).broadcast_to([128, 32]), writes=['Acoef'])
        g.op('act', lambda e: e.activation(out=Acoef[:, :], in_=Acoef[:, :], func=AF.Exp), reads=['Acoef'], writes=['Acoef'])
        g.op('dve', lambda e: e.tensor_scalar_mul(out=Acoef[:, :], in0=Acoef[:, :], scalar1=-1.0), reads=['Acoef'], writes=['Acoef'])

        allct = lambda s: [('uT', s, ct) for ct in range(24)]

        def inproj(i):
            s = i % 2
            xs = i % 2
            x_tile_T(g, nc, c, io['x'], i * T, 2, xin, xb, xT[xs], ptr, ('xin', 'xb', 'xT%d' % xs, 'ptr'), 'dve')
            g.dma('pool', 'SxT', scr['xTs'].rearrange("(k p) t -> p k t", p=128)[:, :, i * T:(i + 1) * T], xT[xs][:, :, :],
                  reads=['xT%d' % xs], writes=[('scr_xTs', i)])
            for cp in range(12):
                pb = pin[cp % 2]
                for half in range(2):
                    ct = cp * 2 + half
                    for kc in range(8):
                        g.op('pe', lambda e, pb=pb, half=half, ct=ct, kc=kc, xs=xs: e.matmul(
                            pb[:, half * T:(half + 1) * T], lhsT=Wa[:, kc, ct * 128:(ct + 1) * 128], rhs=xT[xs][:, kc, :],
                            start=(kc == 0), stop=(kc == 7)), reads=wk('Wa', kc, ct * 128, ct * 128 + 128) + ['xT%d' % xs], writes=['pin%d' % (cp % 2)])
                copy_op(g, ['dve', 'act'][cp % 2], uT[s][:, cp * 2:cp * 2 + 2, 2:2 + T],
                        pb[:, :].rearrange("p (a t) -> p a t", a=2),
                        reads=['pin%d' % (cp % 2)], writes=[('uT', s, cp * 2), ('uT', s, cp * 2 + 1)])
            for j in range(2):
                for kc in range(8):
                    g.op('pe', lambda e, j=j, kc=kc, xs=xs: e.matmul(pdt[:, j * 64:(j + 1) * 64], lhsT=xT[xs][:, kc, j * 128:(j + 1) * 128],
                                                                    rhs=Wa[:, kc, 3072:3136], start=(kc == 0), stop=(kc == 7)),
                         reads=wk('Wa', kc, 3072, 3136) + ['xT%d' % xs], writes=['pdt'])
            g.op('dve', lambda e, i=i: e.tensor_copy(out=dtraw[:, i * 2:i * 2 + 2, :], in_=pdt[:, 0:128].rearrange("p (j n) -> p j n", j=2)),
                 reads=['pdt'], writes=['dtraw'])
            if i == 0:
                g.op('pool', lambda e, s=s: e.memset(uT[s][:, :, 0:2], 0.0), writes=allct(s))
            else:
                sp = (i - 1) % 2
                g.op('pool', lambda e, s=s, sp=sp: e.tensor_copy(out=uT[s][:, :, 0:2], in_=uT[sp][:, :, T:T + 2]),
                     reads=allct(sp), writes=allct(s))
                g.op('pool', lambda e, s=s, sp=sp: e.tensor_copy(out=uT[sp][:, :, T + 2:T + 4], in_=uT[s][:, :, 2:4]),
                     reads=allct(s), writes=allct(sp))
            if i == NT - 1:
                g.op('pool', lambda e, s=s: e.memset(uT[s][:, :, T + 2:T + 4], 0.0), writes=allct(s))

        def conv(i):
            s = i % 2
            bs = bcst[i % 2]
            bk = 'bcst%d' % (i % 2)
            for cp in range(12):
                pb = pcv[cp % 2]
                pk = 'pcv%d' % (cp % 2)
                for half in range(2):
                    ct = cp * 2 + half
                    for k in range(5):
                        g.op('pe', lambda e, pb=pb, half=half, ct=ct, k=k, s=s: e.matmul(
                            pb[:, half * T:(half + 1) * T], lhsT=diag[:, ct, k, :], rhs=uT[s][:, ct, k:k + T],
                            start=(k == 0), stop=(k == 4)), reads=[('uT', s, ct), 'diag'], writes=[pk])
                for half in range(2):
                    ct = cp * 2 + half
                    if ct < 16:
                        g.op('act', lambda e, pb=pb, half=half, ct=ct: e.activation(
                            out=xfm[:, ct, :], in_=pb[:, half * T:(half + 1) * T], func=AF.Silu, bias=wcol[:, ct, 0:1]),
                             reads=[pk, 'wcol'], writes=[('xfm', ct)])
                    else:
                        g.op('act', lambda e, pb=pb, half=half, ct=ct, bs=bs: e.activation(
                            out=bs[:, ct - 16, :], in_=pb[:, half * T:(half + 1) * T], func=AF.Silu, bias=wcol[:, ct, 0:1]),
                             reads=[pk, 'wcol'], writes=[(bk, ct - 16)])
            g.dma('pool', bk, scr['bct'].rearrange("(c p) t -> p c t", p=128)[:, :, i * T:(i + 1) * T], bs[:, :, :],
                  reads=[(bk, q_) for q_ in range(8)], writes=['scr_bct'])
            nt_ = 0
            for j in range(2):
                st = xcst[j]
                for grp, cts in enumerate([range(0, 8), range(8, 16), range(16, 20)]):
                    pt, ptk = ptrc[nt_ % 2], 'ptrc%d' % (nt_ % 2)
                    nt_ += 1
                    for q_, ct in enumerate(cts):
                        if ct < 16:
                            src, rk_ = xfm[:, ct, j * 128:(j + 1) * 128], ('xfm', ct)
                        else:
                            src, rk_ = bs[:, ct - 16, j * 128:(j + 1) * 128], (bk, ct - 16)
                        g.op('pe', lambda e, pt=pt, q_=q_, src=src: e.transpose(out=pt[:, q_, :], in_=src, identity=c['ident'][:, :]),
                             reads=[rk_, 'ident'], writes=[ptk])
                    n_ = len(cts)
                    copy_op(g, ['dve', 'act'][grp % 2], st[:, grp * 1024:grp * 1024 + n_ * 128].rearrange("p (a t) -> p a t", a=n_),
                            pt[:, 0:n_, :], reads=[ptk], writes=['xcst%d' % j])
                t0 = i * T + j * 128
                g.dma('pool', 'xcst%d' % j, scr['xc'][t0:t0 + 128, :], st[:, 0:2048], reads=['xcst%d' % j], writes=['scr_xc'])
                g.dma('pool', 'xcst%d' % j, scr['btok'][t0:t0 + 128, :], st[:, 2048:2560], reads=['xcst%d' % j], writes=['scr_btok'])

        for i in range(NT + 1):
            if i < NT:
                inproj(i)
            if i >= 1:
                conv(i - 1)

        NC_ = S // 128
        NB = 4
        CB_ = NC_ // NB
        bb = lambda t: t[:, :].unsqueeze(1).broadcast_to([128, CB_, 64])
        dsts = scr['dts'].rearrange("(j p) n -> p j n", p=128)
        for blk in range(NB):
            dr = dtraw[:, blk * CB_:(blk + 1) * CB_, :]
            g.op('dve', lambda e, dr=dr: e.tensor_tensor(out=dr, in0=dr, in1=bb(dtb), op=ALU.add), reads=['dtraw', 'dtb'], writes=['dtraw'])
            g.op('act', lambda e, dr=dr: e.activation(out=dtw1[:, :, :], in_=dr, func=AF.Abs), reads=['dtraw'], writes=['dtw1'])
            g.op('act', lambda e: e.activation(out=dtw1[:, :, :], in_=dtw1[:, :, :], func=AF.Exp, scale=-1.0), reads=['dtw1'], writes=['dtw1'])
            g.op('act', lambda e: e.activation(out=dtw1[:, :, :], in_=dtw1[:, :, :], func=AF.Ln, bias=1.0), reads=['dtw1'], writes=['dtw1'])
            g.op('dve', lambda e, dr=dr: e.scalar_tensor_tensor(out=dtout[:, :, 0:64], in0=dr, scalar=0.0, in1=dtw1[:, :, :],
                                                                op0=ALU.max, op1=ALU.add), reads=['dtraw', 'dtw1'], writes=['dtout'])
            g.op('dve', lambda e: e.tensor_tensor(out=dtout[:, :, 64:128], in0=dtout[:, :, 0:64], in1=bb(Acoef), op=ALU.mult),
                 reads=['dtout', 'Acoef'], writes=['dtout'])
            g.dma('sp', 'dtout', dsts[:, blk * CB_:(blk + 1) * CB_, :], dtout[:, :, :], reads=['dtout'], writes=['scr_dts'])
        g.drain('sp')
        g.flush()


def phase_A2(nc, g, c, S, io, scr):
    T = 512
    NT = S // T
    with contextlib.ExitStack() as es:
        sb = lambda name, shape, dt: es.enter_context(nc.sbuf_tensor(name, shape, dt))
        ps = lambda name, shape, dt: es.enter_context(nc.psum_tensor(name, shape, dt))
        Wq = sb("Wq", [128, 8, 2304], BF16)
        wst = [sb("wstq%d" % i, [128, 576], F32) for i in range(4)]
        xin = sb("xin2", [128, 4, 1024], F32)
        xb = sb("xb2", [128, 4, 1024], BF16)
        xT = [sb("xT2%d" % i, [128, 8, T], BF16) for i in range(2)]
        qst = [sb("qst%d" % i, [128, 12, T], BF16) for i in range(2)]
        vst = [sb("vst%d" % i, [128, AW], BF16) for i in range(2)]
        pin = [ps("pq%d" % i, [128, 512], F32) for i in range(3)]
        ptr = ps("ptr2", [128, 8, 128], BF16)
        pv = [ps("pv%d" % i, [128, 512], F32) for i in range(4)]
        rr = RR(['dve', 'act'])
        load_weight_bf16(g, nc, es, Wq, io['w_in'], 0, C_Q, C_Q + 2304, 'Wq', 8, wst, ['wstq%d' % i for i in range(4)], rr, chunk=576)
        for i in range(NT):
            xs = i % 2
            g.dma(['sp', 'pool'][i % 2], 'LxT2', xT[xs][:, :, :], scr['xTs'].rearrange("(k p) t -> p k t", p=128)[:, :, i * T:(i + 1) * T],
                  writes=['xT2%d' % xs])
            qs, qk = qst[i % 2], 'qst%d' % (i % 2)
            for ct in range(12):
                pb, pk = pin[ct % 3], 'pq%d' % (ct % 3)
                for kc in range(8):
                    g.op('pe', lambda e, pb=pb, ct=ct, kc=kc, xs=xs: e.matmul(pb[:, :], lhsT=Wq[:, kc, ct * 128:(ct + 1) * 128],
                                                                            rhs=xT[xs][:, kc, :], start=(kc == 0), stop=(kc == 7)),
                         reads=wk('Wq', kc, ct * 128, ct * 128 + 128) + ['xT2%d' % xs], writes=[pk])
                sc = 0.125 if ct < 6 else 1.0
                if ct % 2 == 0:
                    g.op('act', lambda e, pb=pb, ct=ct, qs=qs, sc=sc: e.mul(out=qs[:, ct, :], in_=pb[:, :], mul=sc), reads=[pk], writes=[qk])
                else:
                    g.op('dve', lambda e, pb=pb, ct=ct, qs=qs, sc=sc: e.tensor_scalar_mul(out=qs[:, ct, :], in0=pb[:, :], scalar1=sc),
                         reads=[pk], writes=[qk])
            g.dma('pool', qk, scr['qkT'].rearrange("(c p) t -> p c t", p=128)[:, :, i * T:(i + 1) * T], qs[:, :, :],
                  reads=[qk], writes=['scr_qkT'])
            for j in range(4):
                vs, vk = vst[j % 2], 'vst%d' % (j % 2)
                for hb, (c0, n) in enumerate([(0, 512), (512, 256)]):
                    pb, pk = pv[(j * 2 + hb) % 4], 'pv%d' % ((j * 2 + hb) % 4)
                    for kc in range(8):
                        g.op('pe', lambda e, pb=pb, kc=kc, xs=xs, j=j, c0=c0, n=n: e.matmul(
                            pb[:, 0:n], lhsT=xT[xs][:, kc, j * 128:(j + 1) * 128], rhs=Wq[:, kc, 1536 + c0:1536 + c0 + n],
                            start=(kc == 0), stop=(kc == 7)), reads=wk('Wq', kc, 1536 + c0, 1536 + c0 + n) + ['xT2%d' % xs], writes=[pk])
                    copy_op(g, ['act', 'dve'][hb], vs[:, c0:c0 + n], pb[:, 0:n], reads=[pk], writes=[vk])
                t0 = i * T + j * 128
                g.dma('pool', vk, scr['v'][t0:t0 + 128, :], vs[:, :], reads=[vk], writes=['scr_v'])
        g.drain('sp')
        g.flush()


def phase_SSD(nc, g, c, S, io, scr, direction):
    NCH = S // 128
    fw = direction == 0
    U = c['Uf'] if fw else c['Ub']
    Uk = 'Uf' if fw else 'Ub'
    mask = c['maskf'] if fw else c['maskb']
    mk = 'maskf' if fw else 'maskb'
    yout = scr['yf'] if fw else scr['yb']
    dcol = 0 if fw else 32
    order = list(range(NCH)) if fw else list(range(NCH - 1, -1, -1))
    with contextlib.ExitStack() as es:
        sb = lambda name, shape, dt: es.enter_context(nc.sbuf_tensor(name, shape, dt))
        ps = lambda name, shape, dt: es.enter_context(nc.psum_tensor(name, shape, dt))
        NSL = 3
        n2 = lambda base, shape, dt: [sb("%s%d_%d" % (base, direction, i), shape, dt) for i in range(NSL)]
        xc = n2("s_xc", [128, DIN], BF16)
        btok = n2("s_btok", [128, 512], BF16)
        bct = n2("s_bct", [128, 8, 128], BF16)
        dts = n2("s_dts", [128, 128], F32)
        negcs = n2("s_negcs", [128, 32], F32)
        ecs = n2("s_ecs", [128, 32], F32)
        dec = n2("s_dec", [128, 32], F32)
        cdec = n2("s_cdec", [128, 32], F32)
        xdt = n2("s_xdt", [128, DIN], BF16)
        xdd = n2("s_xdd", [128, DIN], BF16)
        LT = n2("s_LT", [128, 4, 128], BF16)
        MT = n2("s_MT", [128, 4, 128], BF16)
        CBT = n2("s_CBT", [128, 4, 128], BF16)
        t1 = n2("s_t1", [128, 512], F32)
        yst = n2("s_yst", [128, DIN], BF16)
        hst = sb("s_h%d" % direction, [128, DIN], F32)
        hbf = sb("s_hb%d" % direction, [128, DIN], BF16)
        htmp = sb("s_ht%d" % direction, [128, 512], F32)
        pdiff = [ps("pdiff%d_%d" % (direction, i), [128, 512], F32) for i in range(3)]
        pcb = ps("pcb%d" % direction, [128, 512], F32)
        pms = ps("pms%d" % direction, [128, 512], F32)
        pY = ps("pY%d" % direction, [128, 512], F32)
        pZ = ps("pZ%d" % direction, [128, 512], F32)
        pS = ps("pS%d" % direction, [128, 512], F32)
        tg = 's%d_' % direction
        g.op('pool', lambda e: e.memset(hst[:, :], 0.0), writes=[tg + 'h%d' % gg for gg in range(4)])
        g.op('pool', lambda e: e.memset(hbf[:, :], 0.0), writes=[tg + 'hb%d' % gg for gg in range(4)])
        ndc = [0]

        def prologue(it):
            ch = order[it]
            b = it % NSL
            K_ = lambda nm: tg + nm + str(b)
            t0 = ch * 128
            g.dma('sp', 'Lxc%d' % b, xc[b][:, :], scr['xc'][t0:t0 + 128, :], reads=['scr_xc'], writes=[K_('xc')])
            g.dma('sp', 'Lbtok%d' % b, btok[b][:, :], scr['btok'][t0:t0 + 128, :], reads=['scr_btok'], writes=[K_('btok')])
            g.dma('sp', 'Lbct%d' % b, bct[b][:, :, :], scr['bct'].rearrange("(c p) t -> p c t", p=128)[:, :, t0:t0 + 128],
                  reads=['scr_bct'], writes=[K_('bct')])
            g.dma('sp', 'Ldts%d' % b, dts[b][:, :], scr['dts'][t0:t0 + 128, :], reads=['scr_dts'], writes=[K_('dts')])
            a_ap = dts[b][:, 64 + dcol:64 + dcol + 32]
            dt_ap = dts[b][:, dcol:dcol + 32]
            g.op('pe', lambda e, a_ap=a_ap: e.matmul(pms[:, 0:32], lhsT=U[:, :], rhs=a_ap, start=True, stop=True),
                 reads=[Uk, K_('dts')], writes=[tg + 'pms'])
            g.op('pe', lambda e, a_ap=a_ap: e.matmul(pms[:, 32:64], lhsT=c['onesf'][:, :], rhs=a_ap, start=True, stop=True),
                 reads=['onesf', K_('dts')], writes=[tg + 'pms'])
            g.op('dve', lambda e, b=b: e.tensor_scalar_mul(out=negcs[b][:, :], in0=pms[:, 0:32], scalar1=-1.0),
                 reads=[tg + 'pms'], writes=[K_('negcs')])
            g.op('act', lambda e, b=b: e.activation(out=ecs[b][:, :], in_=pms[:, 0:32], func=AF.Exp), reads=[tg + 'pms'], writes=[K_('ecs')])
            g.op('act', lambda e, b=b: e.activation(out=cdec[b][:, :], in_=pms[:, 32:64], func=AF.Exp), reads=[tg + 'pms'], writes=[K_('cdec')])
            g.op('dve', lambda e, b=b: e.tensor_tensor(out=dec[b][:, :], in0=pms[:, 32:64], in1=negcs[b][:, :], op=ALU.add),
                 reads=[tg + 'pms', K_('negcs')], writes=[K_('dec')])
            g.op('act', lambda e, b=b: e.activation(out=dec[b][:, :], in_=dec[b][:, :], func=AF.Exp), reads=[K_('dec')], writes=[K_('dec')])
            g.op('pool', lambda e, b=b, dt_ap=dt_ap: e.tensor_tensor(
                out=xdt[b][:, :].rearrange("p (h q) -> p h q", q=HP), in0=xc[b][:, :].rearrange("p (h q) -> p h q", q=HP),
                in1=dt_ap.unsqueeze(2).broadcast_to([128, NH, HP]), op=ALU.mult), reads=[K_('xc'), K_('dts')], writes=[K_('xdt')])
            g.op('pool', lambda e, b=b: e.tensor_tensor(
                out=xdd[b][:, :].rearrange("p (h q) -> p h q", q=HP), in0=xdt[b][:, :].rearrange("p (h q) -> p h q", q=HP),
                in1=dec[b][:, :].unsqueeze(2).broadcast_to([128, NH, HP]), op=ALU.mult), reads=[K_('xdt'), K_('dec')], writes=[K_('xdd')])

        def mainpart(it):
            ch = order[it]
            b = it % NSL
            K_ = lambda nm: tg + nm + str(b)
            t0 = ch * 128
            nd = ndc[0]
            for gg in range(4):
                g.op('pe', lambda e, b=b, gg=gg: e.matmul(pcb[:, gg * 128:(gg + 1) * 128], lhsT=bct[b][:, gg, :], rhs=bct[b][:, 4 + gg, :],
                                                        start=True, stop=True), reads=[K_('bct')], writes=[tg + 'pcb'])
            copy_op(g, 'act', CBT[b][:, :, :], pcb[:, :].rearrange("p (a l) -> p a l", a=4), reads=[tg + 'pcb'], writes=[K_('CBT')])
            for gg in range(4):
                for hq in range(2):
                    pd = pdiff[nd % 3]
                    pdk = tg + 'pdiff%d' % (nd % 3)
                    lt, ltk = LT[nd % 3], tg + 'LT%d' % (nd % 3)
                    mt, mtk = MT[nd % 3], tg + 'MT%d' % (nd % 3)
                    nd += 1
                    ndc[0] = nd
                    g.op('pe', lambda e, pd=pd: e.matmul(pd[:, :], lhsT=c['ident'][:, :], rhs=mask[:, :], start=True, stop=False),
                         reads=['ident', mk], writes=[pdk])
                    for hh in range(4):
                        h = gg * 8 + hq * 4 + hh
                        g.op('pe', lambda e, pd=pd, hh=hh, h=h, b=b: e.matmul(
                            pd[:, hh * 128:(hh + 1) * 128], lhsT=dts[b][:, 64 + dcol + h:64 + dcol + h + 1].broadcast_to([128, 128]),
                            rhs=U[:, :], start=False, stop=(hh == 3)), reads=[K_('dts'), Uk], writes=[pdk])
                    for hh in range(4):
                        h = gg * 8 + hq * 4 + hh
                        g.op('act', lambda e, pd=pd, hh=hh, h=h, b=b, lt=lt: e.activation(
                            out=lt[:, hh, :], in_=pd[:, hh * 128:(hh + 1) * 128], func=AF.Exp, bias=negcs[b][:, h:h + 1], scale=1.0),
                             reads=[pdk, K_('negcs')], writes=[ltk])
                    g.op('dve', lambda e, lt=lt, mt=mt, b=b, gg=gg: e.tensor_tensor(
                        out=mt[:, :, :], in0=lt[:, :, :], in1=CBT[b][:, gg:gg + 1, :].broadcast_to([128, 4, 128]), op=ALU.mult),
                         reads=[ltk, K_('CBT')], writes=[mtk])
                    for hh in range(4):
                        h = gg * 8 + hq * 4 + hh
                        hl = hq * 4 + hh
                        g.op('pe', lambda e, mt=mt, hh=hh, h=h, hl=hl, b=b: e.matmul(
                            pY[:, hl * HP:(hl + 1) * HP], lhsT=mt[:, hh, :], rhs=xdt[b][:, h * HP:(h + 1) * HP], start=True, stop=True),
                             reads=[mtk, K_('xdt')], writes=[tg + 'pY'])
                gs = slice(gg * 512, (gg + 1) * 512)
                g.op('pe', lambda e, b=b, gg=gg, gs=gs: e.matmul(pZ[:, :], lhsT=bct[b][:, 4 + gg, :], rhs=hbf[:, gs], start=True, stop=True),
                     reads=[K_('bct'), tg + 'hb%d' % gg], writes=[tg + 'pZ'])
                tt, ttk = t1[gg % 2], tg + 't1%d' % (gg % 2)
                g.op('dve', lambda e, tt=tt, b=b, gg=gg: e.tensor_tensor(
                    out=tt[:, :].rearrange("p (h q) -> p h q", q=HP), in0=pZ[:, :].rearrange("p (h q) -> p h q", q=HP),
                    in1=ecs[b][:, gg * 8:(gg + 1) * 8].unsqueeze(2).broadcast_to([128, 8, HP]), op=ALU.mult),
                     reads=[tg + 'pZ', K_('ecs')], writes=[ttk])
                g.op('dve', lambda e, tt=tt, b=b, gs=gs: e.tensor_tensor(out=yst[b][:, gs], in0=pY[:, :], in1=tt[:, :], op=ALU.add),
                     reads=[tg + 'pY', ttk], writes=[K_('yst')])
                g.op('pe', lambda e, b=b, gg=gg, gs=gs: e.matmul(pS[:, :], lhsT=btok[b][:, gg * 128:(gg + 1) * 128], rhs=xdd[b][:, gs],
                                                               start=True, stop=True), reads=[K_('btok'), K_('xdd')], writes=[tg + 'pS'])
                g.op('pool', lambda e, b=b, gg=gg, gs=gs: e.tensor_tensor(
                    out=htmp[:, :].rearrange("p (h q) -> p h q", q=HP), in0=hst[:, gs].rearrange("p (h q) -> p h q", q=HP),
                    in1=cdec[b][:, gg * 8:(gg + 1) * 8].unsqueeze(2).broadcast_to([128, 8, HP]), op=ALU.mult),
                     reads=[tg + 'h%d' % gg, K_('cdec')], writes=[tg + 'htmp'])
                g.op('dve', lambda e, gs=gs: e.tensor_tensor(out=hst[:, gs], in0=pS[:, :], in1=htmp[:, :], op=ALU.add),
                     reads=[tg + 'pS', tg + 'htmp'], writes=[tg + 'h%d' % gg])
                copy_op(g, 'act', hbf[:, gs], hst[:, gs], reads=[tg + 'h%d' % gg], writes=[tg + 'hb%d' % gg])
            g.dma('pool', 'Syst%d' % b, yout[t0:t0 + 128, :], yst[b][:, :], reads=[K_('yst')], writes=['scr_y%d' % direction])

        SKEW = int(os.environ.get('SSD_SKEW', '0'))
        if SKEW:
            prologue(0)
        for it in range(NCH):
            if SKEW:
                if it + 1 < NCH:
                    prologue(it + 1)
            else:
                prologue(it)
            mainpart(it)
        g.drain('sp')
        g.flush()


def phase_AT(nc, g, c, S, io, scr):
    slopes = [2.0 ** (-8.0 * (h + 1) / 12) for h in range(12)]
    VW = 72
    for gi, dil in enumerate([1, 4, 16]):
        Ls = S // dil
        NQ = Ls // 128
        NM = NQ + 1
        PADT = 64 * dil
        att = scr['att%d' % gi]
        with contextlib.ExitStack() as es:
            sb = lambda name, shape, dt: es.enter_context(nc.sbuf_tensor(name, shape, dt))
            ps = lambda name, shape, dt: es.enter_context(nc.psum_tensor(name, shape, dt))
            QT = sb("QT%d" % gi, [128, 2, S], BF16)
            KT = sb("KT%d" % gi, [128, 2, S + 2 * PADT], BF16)
            V = sb("V%d" % gi, [128, dil * NM, 4, VW], BF16)
            stage = sb("ast%d" % gi, [128, dil * NQ, 4, VW], F32)
            ALB = sb("ALB%d" % gi, [128, 4, 256], F32)
            R = sb("R%d" % gi, [128, 256], F32)
            sT = [sb("sT%d_%d" % (gi, i), [128, 256], F32) for i in range(2)]
            pT = [sb("pT%d_%d" % (gi, i), [128, 256], BF16) for i in range(2)]
            pS = [ps("pS%d_%d" % (gi, i), [128, 512], F32) for i in range(2)]
            pO = [ps("pO%d_%d" % (gi, i), [128, 512], F32) for i in range(2)]
            g.op('pool', lambda e: e.iota(R[:, 0:128], pattern=[[-1, 128]], base=-64, channel_multiplier=1,
                                          allow_small_or_imprecise_dtypes=True), writes=['R'])
            g.op('pool', lambda e: e.iota(R[:, 128:256], pattern=[[-1, 128]], base=64, channel_multiplier=1,
                                          allow_small_or_imprecise_dtypes=True), reads=['R'], writes=['R'])
            g.op('act', lambda e: e.activation(out=R[:, :], in_=R[:, :], func=AF.Abs), reads=['R'], writes=['R'])
            for hl in range(4):
                sl = -slopes[gi * 4 + hl] * dil
                g.op('dve', lambda e, hl=hl, sl=sl: e.tensor_scalar_mul(out=ALB[:, hl, :], in0=R[:, :], scalar1=sl), reads=['R'], writes=['ALB'])
                g.op('pool', lambda e, hl=hl: e.affine_select(out=ALB[:, hl, 0:128], in_=ALB[:, hl, 0:128], pattern=[[-1, 128]], compare_op=ALU.is_ge,
                                                              fill=NEG, base=0, channel_multiplier=1), reads=['ALB'], writes=['ALB'])
                g.op('pool', lambda e, hl=hl: e.affine_select(out=ALB[:, hl, 128:256], in_=ALB[:, hl, 128:256], pattern=[[1, 128]], compare_op=ALU.is_ge,
                                                              fill=NEG, base=0, channel_multiplier=-1), reads=['ALB'], writes=['ALB'])
            g.op('pool', lambda e: e.memset(KT[:, :, 0:PADT], 0.0), writes=['KT'])
            g.op('pool', lambda e: e.memset(KT[:, :, PADT + S:PADT + S + PADT], 0.0), writes=['KT'])
            for hp in range(2):
                r0 = gi * 256 + hp * 128
                NP_ = 4
                for pc in range(NP_):
                    c0, c1 = pc * S // NP_, (pc + 1) * S // NP_
                    g.dma(['sp', 'pool'][pc % 2], 'Lq', QT[:, hp, c0:c1], scr['qkT'][r0:r0 + 128, c0:c1], reads=['scr_qkT'], writes=[('QT', hp, pc)])
                    g.dma(['pool', 'sp'][pc % 2], 'Lk', KT[:, hp, PADT + c0:PADT + c1], scr['qkT'][768 + r0:768 + r0 + 128, c0:c1],
                          reads=['scr_qkT', 'KT'], writes=[('KT', hp, pc)])
            g.op('pool', lambda e: e.memset(V[:, :, :, 64:VW], 0.0), writes=['V'])
            g.op('pool', lambda e: e.memset(V[:, :, :, 64:65], 1.0), reads=['V'], writes=['V'])
            vsrc = scr['v'].rearrange("(j d) c -> d j c", d=dil)
            for r in range(dil):
                g.op('pool', lambda e, r=r: e.memset(V[0:64, r * NM, :, :], 0.0), reads=['V'], writes=['V'])
                g.op('pool', lambda e, r=r: e.memset(V[64:128, r * NM + NM - 1, :, :], 0.0), reads=['V'], writes=['V'])
            for r in range(dil):
                for m in range(NM):
                    lo = 128 * m - 64
                    p0, p1 = (64, 128) if m == 0 else ((0, 64) if m == NM - 1 else (0, 128))
                    i0 = lo + p0
                    g.dma(['sp', 'pool'][(r * NM + m) % 2], 'Lv%d' % ((r * NM + m) % 2), V[p0:p1, r * NM + m, :, 0:64],
                          vsrc[r, i0:i0 + (p1 - p0), gi * 256:(gi + 1) * 256].rearrange("j (h e) -> j h e", e=64),
                          reads=['scr_v', 'V'], writes=[('Vt', r * NM + m)])
            ns = 0
            for hl in range(4):
                hp, po = hl // 2, (hl % 2) * 64
                Kv = KT[po:po + 64, hp, :].rearrange("p (j d) -> p d j", d=dil)
                Qv = QT[po:po + 64, hp, :].rearrange("p (j d) -> p d j", d=dil)
                for r in range(dil):
                    for n in range(NQ):
                        psb, psk = pS[ns % 2], 'pS%d' % (ns % 2)
                        st_, stk = sT[ns % 2], 'sT%d' % (ns % 2)
                        pt_, ptk = pT[ns % 2], 'pT%d' % (ns % 2)
                        pob, pok = pO[ns % 2], 'pO%d' % (ns % 2)
                        ns += 1
                        for half in range(2):
                            m = n + half
                            g.op('pe', lambda e, psb=psb, Kv=Kv, Qv=Qv, r=r, m=m, n=n, half=half: e.matmul(
                                psb[:, half * 128:(half + 1) * 128], lhsT=Kv[:, r, 128 * m:128 * m + 128], rhs=Qv[:, r, 128 * n:128 * n + 128],
                                start=True, stop=True), reads=['KT'] + [('KT', hp, pc) for pc in range(4)] + [('QT', hp, pc) for pc in range(4)], writes=[psk])
                        g.op('dve', lambda e, psb=psb, st_=st_, hl=hl: e.tensor_tensor(
                            out=st_[:, :], in0=psb[:, 0:256], in1=ALB[:, hl, :], op=ALU.add), reads=[psk, 'ALB'], writes=[stk])
                        g.op('act', lambda e, st_=st_, pt_=pt_: e.activation(out=pt_[:, :], in_=st_[:, :], func=AF.Exp), reads=[stk], writes=[ptk])
                        for half in range(2):
                            vt = r * NM + n + half
                            g.op('pe', lambda e, pt_=pt_, pob=pob, vt=vt, hl=hl, half=half: e.matmul(
                                pob[:, 0:VW], lhsT=pt_[:, half * 128:(half + 1) * 128], rhs=V[:, vt, hl, :], start=(half == 0), stop=(half == 1)),
                                 reads=[ptk, 'V', ('Vt', vt)], writes=[pok])
                        copy_op(g, 'dve' if ns % 2 else 'act', stage[:, r * NQ + n, hl, :], pob[:, 0:VW],
                                reads=[pok], writes=[('ast', r * NQ + n)])
            adst = att.rearrange("(j d) c -> d j c", d=dil)
            for r in range(dil):
                for n in range(NQ):
                    g.dma(['sp', 'pool'][n % 2], 'Sat%d' % (n % 2), adst[r, 128 * n:128 * n + 128, :],
                          stage[:, r * NQ + n, :, :].rearrange("p h e -> p (h e)"), reads=[('ast', r * NQ + n)], writes=[('scr_att%d' % gi, r, n)])
            g.drain('sp')
            g.flush()


def layer_norm_ops(g, r, rk, gam, bet, out, ok, junk, jk, st, sk):
    for k in range(2):
        g.op('dve', lambda e, k=k: e.bn_stats(out=st[:, 6 * k:6 * k + 6], in_=r[:, k * 512:(k + 1) * 512]), reads=[rk], writes=[sk])
    g.op('dve', lambda e: e.bn_aggr(out=st[:, 12:14], in_=st[:, 0:12].rearrange("p (a b) -> p a b", b=6)), reads=[sk], writes=[sk])
    g.op('dve', lambda e: e.tensor_scalar_add(out=st[:, 14:15], in0=st[:, 13:14], scalar1=EPS), reads=[sk], writes=[sk])
    g.op('act', lambda e: e.activation(out=st[:, 15:16], in_=st[:, 14:15], func=AF.Sqrt), reads=[sk], writes=[sk])
    g.op('dve', lambda e: e.reciprocal(out=st[:, 16:17], in_=st[:, 15:16]), reads=[sk], writes=[sk])
    g.op('dve', lambda e: e.scalar_tensor_tensor(out=st[:, 17:18], in0=st[:, 12:13], scalar=-1.0, in1=st[:, 16:17], op0=ALU.mult, op1=ALU.mult),
         reads=[sk], writes=[sk])
    g.op('act', lambda e: e.activation(out=out[:, :], in_=r[:, :], func=AF.Identity, bias=st[:, 17:18], scale=st[:, 16:17]),
         reads=[rk, sk], writes=[ok])
    g.op('dve', lambda e: e.tensor_tensor(out=out[:, :], in0=out[:, :], in1=gam[:, :], op=ALU.mult), reads=[ok, 'lng'], writes=[ok])
    g.op('pool', lambda e: e.tensor_tensor(out=out[:, :], in0=out[:, :], in1=bet[:, :], op=ALU.add), reads=[ok, 'lnb'], writes=[ok])


def bcast_row(g, tile, key, src, n, q='pool'):
    g.dma(q, 'cw', tile[:, 0:n], src.rearrange("(o n) -> o n", o=1).broadcast_to([128, n]), writes=[key])


def phase_CZ(nc, g, c, S, io, scr):
    NCH = S // 128
    with contextlib.ExitStack() as es:
        sb = lambda name, shape, dt: es.enter_context(nc.sbuf_tensor(name, shape, dt))
        ps = lambda name, shape, dt: es.enter_context(nc.psum_tensor(name, shape, dt))
        NSL = int(os.environ.get("CZ_SLOTS", "4"))
        n2 = lambda base, shape, dt: [sb("%s_%d" % (base, i), shape, dt) for i in range(NSL)]
        Wz = sb("Wz", [128, 8, 2048], BF16)
        wst = [sb("wstz%d" % i, [128, 512], F32) for i in range(4)]
        Dsk = sb("z_Dsk", [128, 32], F32)
        xin = n2("z_xin", [128, 1, 1024], F32)
        xb = n2("z_xb", [128, 1, 1024], BF16)
        xT = n2("z_xT", [128, 8, 128], BF16)
        yf = n2("z_yf", [128, DIN], BF16)
        yb = n2("z_yb", [128, DIN], BF16)
        xc = n2("z_xc", [128, DIN], BF16)
        ys = n2("z_ys", [128, DIN], F32)
        tmp = n2("z_tmp", [128, DIN], BF16)
        yn = n2("z_yn", [128, DIN], BF16)
        sz = [sb("z_sz%d" % i, [128, 512], F32) for i in range(3)]
        junk = [sb("z_junk%d" % i, [128, 512], F32) for i in range(2)]
        ss = n2("z_ss", [128, 8], F32)
        ptr = [ps("z_ptr%d" % i, [128, 8, 128], BF16) for i in range(2)]
        pz = [ps("z_pz%d" % i, [128, 512], F32) for i in range(4)]
        py = [ps("z_py%d" % i, [128, 512], F32) for i in range(2)]
        rr = RR(['dve', 'act'])
        load_weight_bf16(g, nc, es, Wz, io['w_in'], 0, C_Z, C_Z + 2048, 'Wz', 8, wst, ['wstz%d' % i for i in range(4)], rr, chunk=512)
        bcast_row(g, Dsk, 'Dsk', io['d_skip'], 32)
        nz = 0
        for ch in range(NCH):
            t0 = ch * 128
            b = ch % NSL
            K_ = lambda nm: 'z_%s%d' % (nm, b)
            g.dma('sp', 'Lyf', yf[b][:, :], scr['yf'][t0:t0 + 128, :], writes=[K_('yf')])
            g.dma('sp', 'Lyb', yb[b][:, :], scr['yb'][t0:t0 + 128, :], writes=[K_('yb')])
            g.dma('sp', 'Lxc', xc[b][:, :], scr['xc'][t0:t0 + 128, :], writes=[K_('xc')])
            g.dma('sp', 'LxT', xT[b][:, :, :], scr['xTs'].rearrange("(k p) t -> p k t", p=128)[:, :, t0:t0 + 128], writes=[K_('xT')])
            g.op('pool', lambda e, b=b: e.tensor_tensor(out=tmp[b][:, :].rearrange("p (h q) -> p h q", q=HP), in0=xc[b][:, :].rearrange("p (h q) -> p h q", q=HP),
                                                      in1=Dsk[:, :].unsqueeze(2).broadcast_to([128, NH, HP]), op=ALU.mult),
                 reads=[K_('xc'), 'Dsk'], writes=[K_('tmp')])
            for gg in range(4):
                gs = slice(gg * 512, (gg + 1) * 512)
                pb, pk = pz[nz % 4], 'z_pz%d' % (nz % 4)
                szz, szk = sz[nz % 3], 'z_sz%d' % (nz % 3)
                jk_, jkk = junk[nz % 2], 'z_junk%d' % (nz % 2)
                nz += 1
                for kc in range(8):
                    g.op('pe', lambda e, pb=pb, kc=kc, gs=gs, b=b: e.matmul(pb[:, :], lhsT=xT[b][:, kc, :], rhs=Wz[:, kc, gs], start=(kc == 0), stop=(kc == 7)),
                         reads=[K_('xT')] + wk('Wz', kc, gg * 512, gg * 512 + 512), writes=[pk])
                g.op('act', lambda e, pb=pb, szz=szz: e.activation(out=szz[:, :], in_=pb[:, :], func=AF.Silu), reads=[pk], writes=[szk])
                py_, pyk = py[gg % 2], 'z_py%d' % (gg % 2)
                for q_, (src, sk_) in enumerate([(yf[b], K_('yf')), (yb[b], K_('yb')), (tmp[b], K_('tmp'))]):
                    g.op('pe', lambda e, py_=py_, src=src, gs=gs, q_=q_: e.matmul(py_[:, :], lhsT=c['ident'][:, :], rhs=src[:, gs], start=(q_ == 0), stop=(q_ == 2)),
                         reads=['ident', sk_], writes=[pyk])
                g.op('dve', lambda e, szz=szz, gs=gs, b=b, py_=py_: e.tensor_tensor(out=ys[b][:, gs], in0=py_[:, :], in1=szz[:, :], op=ALU.mult),
                     reads=[pyk, szk], writes=[K_('ys')])
                g.op('dve', lambda e, gs=gs, gg=gg, b=b, jk_=jk_: e.scalar_tensor_tensor(out=jk_[:, :], in0=ys[b][:, gs], scalar=1.0, in1=ys[b][:, gs],
                                                                                      op0=ALU.mult, op1=ALU.mult, accum_out=ss[b][:, gg:gg + 1]),
                     reads=[K_('ys')], writes=[jkk, K_('ss')])
            g.op('dve', lambda e, b=b: e.tensor_scalar(out=ss[b][:, 0:4], in0=ss[b][:, 0:4], scalar1=1.0 / 512, scalar2=EPS, op0=ALU.mult, op1=ALU.add),
                 reads=[K_('ss')], writes=[K_('ss')])
            g.op('act', lambda e, b=b: e.activation(out=ss[b][:, 0:4], in_=ss[b][:, 0:4], func=AF.Sqrt), reads=[K_('ss')], writes=[K_('ss')])
            g.op('dve', lambda e, b=b: e.reciprocal(out=ss[b][:, 4:8], in_=ss[b][:, 0:4]), reads=[K_('ss')], writes=[K_('ss')])
            for gg in range(4):
                gs = slice(gg * 512, (gg + 1) * 512)
                g.op('act', lambda e, gs=gs, gg=gg, b=b: e.mul(out=yn[b][:, gs], in_=ys[b][:, gs], mul=ss[b][:, 4 + gg:5 + gg]),
                     reads=[K_('ys'), K_('ss')], writes=[K_('yn')])
            g.dma('pool', 'Syn', scr['yn'][t0:t0 + 128, :], yn[b][:, :], reads=[K_('yn')], writes=[('scr_yn', ch)])
        g.drain('sp')
        g.flush()


def phase_C(nc, g, c, S, io, scr):
    NCH = S // 128
    with contextlib.ExitStack() as es:
        sb = lambda name, shape, dt: es.enter_context(nc.sbuf_tensor(name, shape, dt))
        ps = lambda name, shape, dt: es.enter_context(nc.psum_tensor(name, shape, dt))
        n2 = lambda base, shape, dt: [sb("%s_%d" % (base, i), shape, dt) for i in range(2)]
        Wg = sb("Wg", [128, 8, 2048], BF16)
        Wps = sb("Wps", [128, 16, 1024], BF16)
        Wo = sb("Wo", [128, 8, 1024], BF16)
        Wpa = sb("Wpa", [128, 2, 1024], BF16)
        wst = [sb("wstc%d" % i, [128, 512], F32) for i in range(4)]
        nwrow = sb("nwrow", [16, 128], F32)
        nwcol = sb("nwcol", [128, 16], F32)
        bgf = sb("bgf", [1, 2048], F32)
        bgb = sb("bgb", [1, 2048], BF16)
        lng = sb("lng1", [128, 1024], F32)
        lnb = sb("lnb1", [128, 1024], F32)
        xin = n2("c_xin", [128, 1, 1024], F32)
        xb = n2("c_xb", [128, 1, 1024], BF16)
        xT = n2("c_xT", [128, 8, 128], BF16)
        yn = n2("c_yn", [128, DIN], BF16)
        ynT = n2("c_ynT", [128, 16, 128], BF16)
        at = [n2("c_at%d" % i, [128, 4, 72], F32) for i in range(3)]
        ya = n2("c_ya", [128, 4, 64], BF16)
        yaT = n2("c_yaT", [128, 2, 128], BF16)
        sg = [sb("c_sg%d" % i, [128, 512], F32) for i in range(3)]
        mm = n2("c_m", [128, 1024], F32)
        mt2 = n2("c_m2", [128, 512], F32)
        mb = n2("c_mb", [128, 1024], BF16)
        mT = n2("c_mT", [128, 8, 128], BF16)
        rr_ = n2("c_r", [128, 1024], F32)
        h1 = n2("c_h1", [128, 1024], F32)
        st = n2("c_st", [128, 20], F32)
        ptr = [ps("c_ptr%d" % i, [128, 8, 128], BF16) for i in range(2)]
        pA = [ps("c_pA%d" % i, [128, 512], F32) for i in range(3)]
        pG = [ps("c_pG%d" % i, [128, 512], F32) for i in range(3)]
        rr = RR(['dve', 'act'])
        load_weight_bf16(g, nc, es, Wg, io['w_in'], 0, C_G, C_G + 2048, 'Wg', 8, wst, ['wstc%d' % i for i in range(4)], rr, chunk=512)
        load_weight_bf16(g, nc, es, Wo, io['w_out'], 0, 0, 1024, 'Wo', 8, wst, ['wstc%d' % i for i in range(4)], rr, chunk=512)
        load_weight_bf16(g, nc, es, Wpa, io['w_proj_attn'], 0, 0, 1024, 'Wpa', 2, wst, ['wstc%d' % i for i in range(4)], rr, chunk=512)
        g.dma('pool', 'cw', nwrow[:, :], io['ssd_norm_w'].rearrange("(k p) -> k p", p=128), writes=['nwrow'])
        g.op('pe', lambda e: e.matmul(pA[0][:, 0:16], lhsT=nwrow[0:16, :], rhs=c['identf'][0:16, 0:16], start=True, stop=True),
             reads=['nwrow', 'identf'], writes=['c_pA0'])
        g.op('dve', lambda e: e.tensor_copy(out=nwcol[:, :], in_=pA[0][:, 0:16]), reads=['c_pA0'], writes=['nwcol'])
        for kc in range(16):
            for hf in range(2):
                wi = (kc * 2 + hf) % 4
                sk = 'wstc%d' % wi
                g.dma(['sp', 'pool'][wi % 2], 'wst' + sk, wst[wi][:, :], io['w_proj_ssd'][kc * 128:(kc + 1) * 128, hf * 512:(hf + 1) * 512], writes=[sk])
                g.op('dve', lambda e, kc=kc, hf=hf, wi=wi: e.tensor_scalar_mul(out=Wps[:, kc, hf * 512:(hf + 1) * 512], in0=wst[wi][:, :],
                                                                                    scalar1=nwcol[:, kc:kc + 1]), reads=[sk, 'nwcol'], writes=['Wps'])
        bcast_row(g, lng, 'lng', io['ln1_g'], 1024)
        bcast_row(g, lnb, 'lnb', io['ln1_b'], 1024)
        g.dma('pool', 'cw', bgf[0:1, :], io['b_gate'].rearrange("(o n) -> o n", o=1), writes=['bgf'])
        g.op('dve', lambda e: e.tensor_copy(out=bgb[:, :], in_=bgf[:, :]), reads=['bgf'], writes=['bgb'])
        na = 0
        ng = 0
        for ch in range(NCH):
            t0 = ch * 128
            b = ch % 2
            K_ = lambda nm: 'c_%s%d' % (nm, b)
            g.dma('sp', 'Lyn', yn[b][:, :], scr['yn'][t0:t0 + 128, :], writes=[K_('yn')])
            for i in range(3):
                g.dma('sp', 'Lat%d' % i, at[i][b][:, :, :], scr['att%d' % i][t0:t0 + 128, :].rearrange("p (h e) -> p h e", e=72),
                      writes=[K_('at%d' % i)])
            g.dma('sp', 'LxT', xT[b][:, :, :], scr['xTs'].rearrange("(k p) t -> p k t", p=128)[:, :, t0:t0 + 128], writes=[K_('xT')])
            g.dma('pool', 'Lxin', xin[b][:, 0, :], io['x'][t0:t0 + 128, :], writes=[K_('xin')])
            for hh in range(2):
                for k8 in range(8):
                    kc = hh * 8 + k8
                    g.op('pe', lambda e, k8=k8, kc=kc, b=b: e.transpose(out=ptr[b][:, k8, :], in_=yn[b][:, kc * 128:(kc + 1) * 128], identity=c['ident'][:, :]),
                         reads=[K_('yn'), 'ident'], writes=[K_('ptr')])
                copy_op(g, ['act', 'dve'][hh], ynT[b][:, hh * 8:(hh + 1) * 8, :], ptr[b][:, :, :], reads=[K_('ptr')], writes=[K_('ynT')])
            for half in range(2):
                cs_ = slice(half * 512, (half + 1) * 512)
                pa, pak = pA[na % 3], 'c_pA%d' % (na % 3)
                na += 1
                pg, pgk = pG[ng % 3], 'c_pG%d' % (ng % 3)
                sgt, sgk = sg[ng % 3], 'c_sg%d' % (ng % 3)
                ng += 1
                for kc in range(16):
                    g.op('pe', lambda e, pa=pa, kc=kc, cs_=cs_, b=b: e.matmul(pa[:, :], lhsT=ynT[b][:, kc, :], rhs=Wps[:, kc, cs_],
                                                                            start=(kc == 0), stop=(kc == 15)), reads=[K_('ynT'), 'Wps'], writes=[pak])
                for kc in range(8):
                    g.op('pe', lambda e, pg=pg, kc=kc, cs_=cs_, b=b: e.matmul(pg[:, :], lhsT=xT[b][:, kc, :], rhs=Wg[:, kc, cs_],
                                                                            start=(kc == 0), stop=False), reads=[K_('xT')] + wk('Wg', kc, half * 512, half * 512 + 512), writes=[pgk])
                g.op('pe', lambda e, pg=pg, cs_=cs_: e.matmul(pg[:, :], lhsT=c['onesb'][0:1, 0:128], rhs=bgb[0:1, cs_], start=False, stop=True),
                     reads=['onesb', 'bgb'], writes=[pgk])
                g.op('act', lambda e, pg=pg, sgt=sgt: e.activation(out=sgt[:, :], in_=pg[:, :], func=AF.Sigmoid), reads=[pgk], writes=[sgk])
                g.op('dve', lambda e, pa=pa, sgt=sgt, cs_=cs_, b=b: e.tensor_tensor(out=mm[b][:, cs_], in0=pa[:, :], in1=sgt[:, :], op=ALU.mult),
                     reads=[pak, sgk], writes=[K_('m')])
            g.op('pool', lambda e, b=b: e.tensor_tensor(out=at[0][b][:, :, 0:65], in0=at[0][b][:, :, 0:65], in1=at[1][b][:, :, 0:65], op=ALU.add),
                 reads=[K_('at0'), K_('at1')], writes=[K_('at0')])
            g.op('pool', lambda e, b=b: e.tensor_tensor(out=at[0][b][:, :, 0:65], in0=at[0][b][:, :, 0:65], in1=at[2][b][:, :, 0:65], op=ALU.add),
                 reads=[K_('at0'), K_('at2')], writes=[K_('at0')])
            g.op('dve', lambda e, b=b: e.reciprocal(out=at[1][b][:, :, 64:65], in_=at[0][b][:, :, 64:65]), reads=[K_('at0'), K_('at1')], writes=[K_('at1')])
            g.op('dve', lambda e, b=b: e.tensor_tensor(out=ya[b][:, :, :], in0=at[0][b][:, :, 0:64], in1=at[1][b][:, :, 64:65].broadcast_to([128, 4, 64]), op=ALU.mult),
                 reads=[K_('at0'), K_('at1')], writes=[K_('ya')])
            for k2 in range(2):
                g.op('pe', lambda e, k2=k2, b=b: e.transpose(out=ptr[b][:, k2, :], in_=ya[b][:, 2 * k2:2 * k2 + 2, :].rearrange("p h e -> p (h e)"),
                                                             identity=c['ident'][:, :]), reads=[K_('ya'), 'ident'], writes=[K_('ptr')])
            copy_op(g, 'act', yaT[b][:, :, :], ptr[b][:, 0:2, :], reads=[K_('ptr')], writes=[K_('yaT')])
            for half in range(2):
                cs_ = slice(half * 512, (half + 1) * 512)
                gc_ = slice(1024 + half * 512, 1024 + (half + 1) * 512)
                pa, pak = pA[na % 3], 'c_pA%d' % (na % 3)
                na += 1
                pg, pgk = pG[ng % 3], 'c_pG%d' % (ng % 3)
                sgt, sgk = sg[ng % 3], 'c_sg%d' % (ng % 3)
                ng += 1
                for kc in range(2):
                    g.op('pe', lambda e, pa=pa, kc=kc, cs_=cs_, b=b: e.matmul(pa[:, :], lhsT=yaT[b][:, kc, :], rhs=Wpa[:, kc, cs_],
                                                                            start=(kc == 0), stop=(kc == 1)), reads=[K_('yaT')] + wk('Wpa', kc, half * 512, half * 512 + 512), writes=[pak])
                for kc in range(8):
                    g.op('pe', lambda e, pg=pg, kc=kc, gc_=gc_, b=b: e.matmul(pg[:, :], lhsT=xT[b][:, kc, :], rhs=Wg[:, kc, gc_],
                                                                            start=(kc == 0), stop=False), reads=[K_('xT')] + wk('Wg', kc, 1024 + half * 512, 1024 + half * 512 + 512), writes=[pgk])
                g.op('pe', lambda e, pg=pg, gc_=gc_: e.matmul(pg[:, :], lhsT=c['onesb'][0:1, 0:128], rhs=bgb[0:1, gc_], start=False, stop=True),
                     reads=['onesb', 'bgb'], writes=[pgk])
                g.op('act', lambda e, pg=pg, sgt=sgt: e.activation(out=sgt[:, :], in_=pg[:, :], func=AF.Sigmoid), reads=[pgk], writes=[sgk])
                g.op('dve', lambda e, pa=pa, sgt=sgt, b=b: e.tensor_tensor(out=mt2[b][:, :], in0=pa[:, :], in1=sgt[:, :], op=ALU.mult),
                     reads=[pak, sgk], writes=[K_('m2')])
                g.op('dve', lambda e, cs_=cs_, b=b: e.tensor_tensor(out=mb[b][:, cs_], in0=mm[b][:, cs_], in1=mt2[b][:, :], op=ALU.add),
                     reads=[K_('m'), K_('m2')], writes=[K_('mb')])
            for k8 in range(8):
                g.op('pe', lambda e, k8=k8, b=b: e.transpose(out=ptr[b][:, k8, :], in_=mb[b][:, k8 * 128:(k8 + 1) * 128], identity=c['ident'][:, :]),
                     reads=[K_('mb'), 'ident'], writes=[K_('ptr')])
            copy_op(g, 'act', mT[b][:, :, :], ptr[b][:, :, :], reads=[K_('ptr')], writes=[K_('mT')])
            for half in range(2):
                cs_ = slice(half * 512, (half + 1) * 512)
                pa, pak = pA[na % 3], 'c_pA%d' % (na % 3)
                na += 1
                for kc in range(8):
                    g.op('pe', lambda e, pa=pa, kc=kc, cs_=cs_, b=b: e.matmul(pa[:, :], lhsT=mT[b][:, kc, :], rhs=Wo[:, kc, cs_],
                                                                            start=(kc == 0), stop=(kc == 7)), reads=[K_('mT')] + wk('Wo', kc, half * 512, half * 512 + 512), writes=[pak])
                g.op('dve', lambda e, pa=pa, cs_=cs_, b=b: e.scalar_tensor_tensor(out=rr_[b][:, cs_], in0=xin[b][:, 0, cs_], scalar=ALPHA, in1=pa[:, :],
                                                                               op0=ALU.mult, op1=ALU.add), reads=[K_('xin'), pak], writes=[K_('r')])
            layer_norm_ops(g, rr_[b], K_('r'), lng, lnb, h1[b], K_('h1'), h1[b], K_('h1'), st[b], K_('st'))
            g.dma('pool', 'Sh1', scr['h1'][t0:t0 + 128, :], h1[b][:, :], reads=[K_('h1')], writes=[('scr_h1', ch)])
        g.drain('sp')
        g.flush()


def phase_D(nc, g, c, S, io, scr):
    T = 256
    NT = S // T
    with contextlib.ExitStack() as es:
        sb = lambda name, shape, dt: es.enter_context(nc.sbuf_tensor(name, shape, dt))
        ps = lambda name, shape, dt: es.enter_context(nc.psum_tensor(name, shape, dt))
        Wu = sb("Wu", [128, 8, DFF], BF16)
        Wd = sb("Wd", [128, 32, D], BF16)
        wst = [sb("wstd%d" % i, [128, 512], F32) for i in range(4)]
        lng = sb("lng2", [128, 1024], F32)
        lnb = sb("lnb2", [128, 1024], F32)
        hin = [sb("d_hin%d" % i, [128, 2, 1024], F32) for i in range(2)]
        hb = [sb("d_hb0", [128, 2, 1024], BF16)] * 2
        hT = [sb("d_hT%d" % i, [128, 8, T], BF16) for i in range(2)]
        f1 = sb("d_f1", [128, 32, T], BF16)
        rl = [sb("d_rl%d" % i, [128, 512], F32) for i in range(2)]
        r2 = [sb("d_r20", [128, 1024], F32)] * 2
        o2 = [sb("d_o0", [128, 1024], F32)] * 2
        st = [sb("d_st0", [128, 20], F32)] * 2
        ptr = [ps("d_ptr%d" % i, [128, 8, 128], BF16) for i in range(2)]
        pu = [ps("d_pu%d" % i, [128, 512], F32) for i in range(3)]
        pd = [ps("d_pd%d" % i, [128, 512], F32) for i in range(2)]
        rr = RR(['dve', 'act'])
        load_weight_bf16(g, nc, es, Wu, io['w_up'], 0, 0, DFF, 'Wu', 8, wst, ['wstd%d' % i for i in range(4)], rr, chunk=512)
        load_weight_bf16(g, nc, es, Wd, io['w_down'], 0, 0, D, 'Wd', 32, wst, ['wstd%d' % i for i in range(4)], rr, chunk=512)
        bcast_row(g, lng, 'lng', io['ln2_g'], 1024)
        bcast_row(g, lnb, 'lnb', io['ln2_b'], 1024)
        for i in range(NT):
            hs = i % 2
            x_tile_T(g, nc, c, scr['h1'], i * T, 2, hin[hs], hb[hs], hT[hs], ptr[hs], ('d_hin%d' % hs, 'd_hb0', 'd_hT%d' % hs, 'd_ptr%d' % hs), 'dve')
            for fp in range(16):
                pb, pk = pu[fp % 3], 'd_pu%d' % (fp % 3)
                for half in range(2):
                    ft = fp * 2 + half
                    for kc in range(8):
                        g.op('pe', lambda e, pb=pb, half=half, ft=ft, kc=kc, hs=hs: e.matmul(pb[:, half * T:(half + 1) * T], lhsT=Wu[:, kc, ft * 128:(ft + 1) * 128],
                                                                                rhs=hT[hs][:, kc, :], start=(kc == 0), stop=(kc == 7)),
                             reads=wk('Wu', kc, ft * 128, ft * 128 + 128) + ['d_hT%d' % hs], writes=[pk])
                rt, rk = rl[fp % 2], 'd_rl%d' % (fp % 2)
                g.op('act', lambda e, pb=pb, rt=rt: e.activation(out=rt[:, :], in_=pb[:, :], func=AF.Relu), reads=[pk], writes=[rk])
                g.op('dve', lambda e, rt=rt, fp=fp: e.tensor_tensor(
                    out=f1[:, fp * 2:fp * 2 + 2, :], in0=rt[:, :].rearrange("p (a t) -> p a t", a=2), in1=rt[:, :].rearrange("p (a t) -> p a t", a=2),
                    op=ALU.mult), reads=[rk], writes=['d_f1'])
            for j in range(2):
                for half in range(2):
                    cs_ = slice(half * 512, (half + 1) * 512)
                    for kc in range(32):
                        g.op('pe', lambda e, half=half, kc=kc, j=j, cs_=cs_: e.matmul(pd[half][:, :], lhsT=f1[:, kc, j * 128:(j + 1) * 128], rhs=Wd[:, kc, cs_],
                                                                                  start=(kc == 0), stop=(kc == 31)), reads=['d_f1'] + wk('Wd', kc, half * 512, half * 512 + 512), writes=['d_pd%d' % half])
                    g.op('dve', lambda e, half=half, j=j, cs_=cs_, hs=hs: e.scalar_tensor_tensor(out=r2[j][:, cs_], in0=hin[hs][:, j, cs_], scalar=ALPHA, in1=pd[half][:, :],
                                                                                       op0=ALU.mult, op1=ALU.add), reads=['d_hin%d' % hs, 'd_pd%d' % half], writes=['d_r20'])
                ob, obk = o2[j], 'd_o0'
                layer_norm_ops(g, r2[j], 'd_r20', lng, lnb, ob, obk, ob, obk, st[j], 'd_st0')
                t0 = i * T + j * 128
                g.dma('pool', 'Sout%d' % j, io['out'][t0:t0 + 128, :], ob[:, :], reads=[obk], writes=[('out', t0)])
        g.drain('sp')
        g.flush()


def build(S=SEQ, phases=('A1',), debug=False):
    nc = bass.Bass("TRN2", target_bir_lowering=False)
    io = {}
    specs = [("x", [S, D]), ("w_in", [D, INC]), ("b_gate", [2 * D]), ("conv_w", [5, CONV]), ("conv_b", [CONV]),
             ("dt_bias_f", [NH]), ("dt_bias_b", [NH]), ("a_log_f", [NH]), ("a_log_b", [NH]), ("d_skip", [NH]),
             ("ssd_norm_w", [DIN]), ("w_proj_ssd", [DIN, D]), ("w_proj_attn", [256, D]), ("w_out", [D, D]),
             ("ln1_g", [D]), ("ln1_b", [D]), ("w_up", [D, DFF]), ("w_down", [DFF, D]), ("ln2_g", [D]), ("ln2_b", [D])]
    for n, shp in specs:
        io[n] = nc.dram_tensor(n, shp, F32, kind="ExternalInput").ap()
    out = nc.dram_tensor("out", [S, D], F32, kind="ExternalOutput").ap()
    io['out'] = out
    kind = "ExternalOutput" if debug else "Internal"
    scr = {}
    for n, shp, dt in [("xc", [S, DIN], BF16), ("btok", [S, 512], BF16), ("bct", [1024, S], BF16), ("dts", [S, 128], F32),
                       ("qkT", [1536, S], BF16), ("v", [S, AW], BF16), ("yf", [S, DIN], BF16), ("yb", [S, DIN], BF16),
                       ("att0", [S, 288], F32), ("att1", [S, 288], F32), ("att2", [S, 288], F32), ("h1", [S, D], F32), ("yn", [S, DIN], BF16), ("xTs", [D, S], BF16)]:
        scr[n] = nc.dram_tensor("scr_" + n, shp, dt, kind=kind).ap()
    with contextlib.ExitStack() as es:
        g = G(nc, es)
        c = make_consts(nc, g, es)
        if 'A1' in phases:
            phase_A1(nc, g, c, S, io, scr)
        if 'A2' in phases:
            phase_A2(nc, g, c, S, io, scr)
        if 'SF' in phases:
            phase_SSD(nc, g, c, S, io, scr, 0)
        if 'SB' in phases:
            phase_SSD(nc, g, c, S, io, scr, 1)
        if 'AT' in phases:
            phase_AT(nc, g, c, S, io, scr)
        if 'C' in phases or 'CZ' in phases:
            phase_CZ(nc, g, c, S, io, scr)
        if 'C' in phases or 'CC' in phases:
            phase_C(nc, g, c, S, io, scr)
        if 'D' in phases:
            phase_D(nc, g, c, S, io, scr)
        g.drain('sp')
        g.flush()
        print("instructions:", g.nins)
    return nc


ALL_PHASES = ('A1', 'A2', 'SF', 'SB', 'AT', 'C', 'D')


def kernel(**inputs):
    x = np.asarray(inputs['x'], dtype=np.float32)
    nb = x.shape[0]
    nc = build(S=SEQ, phases=ALL_PHASES, debug=False)
    shared = {k: np.ascontiguousarray(np.asarray(v, dtype=np.float32)) for k, v in inputs.items() if k != 'x'}
    in_maps = []
    for b in range(nb):
        m = dict(shared)
        m['x'] = np.ascontiguousarray(x[b])
        in_maps.append(m)
    res = run_bass_kernel_spmd(nc, in_maps, core_ids=list(range(nb)))
    return np.stack([np.asarray(r['out'], dtype=np.float32) for r in res.results], axis=0)
```

```python
import contextlib
import math
import os
import numpy as np
import concourse.bass as bass
import concourse.mybir as mybir
from concourse.bass_utils import run_bass_kernel_spmd

F32 = mybir.dt.float32
BF16 = mybir.dt.bfloat16
ALU = mybir.AluOpType
AF = mybir.ActivationFunctionType
AX = mybir.AxisListType

ENGS = ['sp', 'act', 'dve', 'pool', 'pe']
STRICT_ENG = ('pool', 'dve')

D = 1024
SEQ = 8192
NCORES = 8
DIN = 2048
NH = 32
HP = 64
NG = 4
DS = 128
CONV = 3072
AW = 768
DFF = 4096
C_Z, C_X, C_DT, C_Q, C_K, C_V, C_G = 0, 2048, 5120, 5184, 5952, 6720, 7488
INC = 9536
EPS = 1e-5
ALPHA = 2.0 ** 0.25
NEG = -30000.0


class _Rec:
    def __init__(self):
        self.name = None
        self.kw = {}
        self.args = ()

    def __getattr__(self, name):
        def f(*args, **kw):
            self.name = name
            self.kw = kw
            self.args = args
            return self
        return f

    def then_inc(self, *a, **k):
        return self


def _free_size(ap):
    try:
        shp = ap.shape
        n = 1
        for d_ in shp[1:]:
            n *= int(d_)
        return n
    except Exception:
        return 256


NDMASEM = 14
SAME_CLASS_RELAX = int(os.environ.get('SAME_CLASS_RELAX', '0'))
ACT_SPLIT = int(os.environ.get('ACT_SPLIT', '1'))
SYNC_NS = float(os.environ.get('G_SYNC', '2000'))
import os
WINDOW = int(os.environ.get('G_WINDOW', '96'))


class G:
    def __init__(self, nc, es):
        self.nc = nc
        self.es = es
        self.sems = {}
        self.cnt = {}
        self.waited = {e: {} for e in ENGS}
        self.nodes = []
        self.lastw = {}
        self.readers = {}
        self.nins = 0
        self.drain_eng = None
        self.dma_i = {}

    def sem(self, key):
        if key not in self.sems:
            self.sems[key] = self.es.enter_context(self.nc.semaphore(key))
            self.cnt[key] = 0
        return self.sems[key]

    def _preds(self, reads, writes):
        ps = set()
        for r in reads:
            w = self.lastw.get(r)
            if w is not None:
                ps.add(w)
        for w_ in writes:
            w = self.lastw.get(w_)
            if w is not None:
                ps.add(w)
            for d in self.readers.get(w_, ()):
                ps.add(d)
        return ps

    def _mark(self, me, reads, writes):
        for w in writes:
            self.lastw[w] = me
            self.readers[w] = []
        for r in reads:
            self.readers.setdefault(r, []).append(me)

    def _cost(self, eng, fn):
        rec = _Rec()
        try:
            fn(rec)
        except Exception:
            return 300.0
        kw = rec.kw
        out = kw.get('out', rec.args[0] if rec.args else None)
        n = _free_size(out) if out is not None else 256
        if eng == 'pe':
            if rec.name == 'transpose':
                return 75.0
            rhs = kw.get('rhs')
            nn = _free_size(rhs) if rhs is not None else n
            passes = 4.0 if (rhs is not None and rhs.dtype == F32) else 1.0
            return max(nn, 64) * passes / 2.4 + 60.0
        if eng == 'act':
            extra = 0.0
            for k_ in ('bias', 'scale', 'accum_out'):
                v = kw.get(k_)
                if v is not None and not isinstance(v, (int, float)):
                    extra += 93.0
            return 160.0 + n / 1.4 + extra
        if eng == 'dve':
            return 70.0 + n / 0.96
        if eng == 'pool':
            return 120.0 + n * 1.6
        return 100.0

    def _semkey(self, eng, fn):
        if eng not in ('act', 'dve') or not ACT_SPLIT:
            return eng
        rec = _Rec()
        try:
            fn(rec)
            srcs = [rec.kw.get(k_) for k_ in ('in_', 'in0', 'in1', 'data', 'on_true', 'on_false')]
            if len(rec.args) > 1:
                srcs.append(rec.args[1])
            for src in srcs:
                if src is not None and hasattr(src, 'space') and 'PSUM' in str(src.space).upper():
                    return eng + '_p'
            return eng
        except Exception:
            return eng

    def op(self, eng, fn, reads=(), writes=()):
        preds = self._preds(reads, writes)
        i = len(self.nodes)
        self.nodes.append(dict(eng=eng, fn=fn, preds=preds, key=self._semkey(eng, fn), inc=1, cost=self._cost(eng, fn), lat=0.0))
        self._mark(i, reads, writes)
        self.nins += 1

    def dma(self, queue, stream, out, in_, reads=(), writes=(), **kw):
        preds = self._preds(reads, writes)
        i = len(self.nodes)
        nbytes = 0
        try:
            nbytes = out.nbytes()
        except Exception:
            nbytes = 65536
        self.nodes.append(dict(eng=queue, fn=(lambda e: e.dma_start(out=out, in_=in_, **kw)), preds=preds, key='d_' + stream, inc=16,
                               cost=60.0, lat=2000.0 + nbytes / 60.0))
        self._mark(i, reads, writes)
        self.nins += 1

    def drain(self, eng='sp'):
        self.drain_eng = eng

    def _schedule(self):
        nodes = self.nodes
        n = len(nodes)
        chain = [c_ for c_ in os.environ.get('G_CHAIN', '').split(',') if c_]
        last = {}
        for i, nd in enumerate(nodes):
            if nd['eng'] in chain:
                if nd['eng'] in last:
                    nd['preds'] = set(nd['preds']) | {last[nd['eng']]}
                last[nd['eng']] = i
        succ = [[] for _ in range(n)]
        indeg = [0] * n
        for i, nd in enumerate(nodes):
            nd['preds'] = [p for p in nd['preds'] if p < n]
            indeg[i] = len(nd['preds'])
            for p in nd['preds']:
                succ[p].append(i)
        import heapq
        ready = [i for i in range(n) if indeg[i] == 0]
        heapq.heapify(ready)
        finish = [0.0] * n
        eng_free = {e: 0.0 for e in ENGS}
        last_on = {}
        order = []
        while ready:
            cand = heapq.nsmallest(WINDOW, ready)
            best, best_t = None, None
            for i in cand:
                nd = nodes[i]
                t = eng_free[nd['eng']]
                for p in nd['preds']:
                    tp = finish[p] + (SYNC_NS if nodes[p]['eng'] != nd['eng'] or nodes[p]['inc'] == 16 else 0.0)
                    if tp > t:
                        t = tp
                if best is None or t < best_t - 1e-9:
                    best, best_t = i, t
            ready.remove(best)
            heapq.heapify(ready)
            nd = nodes[best]
            why = ('eng', last_on.get(nd['eng']))
            tb = eng_free[nd['eng']]
            for p in nd['preds']:
                tp = finish[p] + (SYNC_NS if nodes[p]['eng'] != nd['eng'] or nodes[p]['inc'] == 16 else 0.0)
                if tp > tb:
                    tb = tp
                    why = ('dep', p)
            nd['why'] = why
            nd['t0'] = best_t
            last_on[nd['eng']] = best
            eng_free[nd['eng']] = best_t + nd['cost']
            finish[best] = best_t + nd['cost'] + nd['lat']
            order.append(best)
            for s_ in succ[best]:
                indeg[s_] -= 1
                if indeg[s_] == 0:
                    heapq.heappush(ready, s_)
        assert len(order) == n
        self.sim_ns = max(finish) if n else 0.0
        busy = {e: 0.0 for e in ENGS}
        for nd in nodes:
            busy[nd['eng']] += nd['cost']
        self.busy = busy
        if os.environ.get('G_CRIT') and n:
            i = max(range(n), key=lambda j: finish[j])
            chain = []
            while i is not None and len(chain) < 400:
                nd = nodes[i]
                chain.append((i, nd['eng'], nd['why'][0], round(nd['t0'] / 1000.0, 1), round(nd['cost'])))
                i = nd['why'][1]
            tot = {}
            for (_, e_, w_, _, c_) in chain:
                tot[(e_, w_)] = tot.get((e_, w_), 0) + c_
            print("CRIT tail:", chain[:60])
            print("CRIT totals(ns) over last %d nodes:" % len(chain), tot)
        return order

    def flush(self):
        nc = self.nc
        nodes = self.nodes
        order = self._schedule()
        q = {e: [] for e in ENGS}
        semval = {}
        for i in order:
            nd = nodes[i]
            eng = nd['eng']
            deps = {}
            for p in nd['preds']:
                pk, pv = semval[p]
                pn = nodes[p]
                if pn['eng'] == eng and pn['inc'] == 1 and (eng == 'pe' or eng not in STRICT_ENG or (SAME_CLASS_RELAX and eng == 'dve' and pn['key'] == nd['key'])):
                    continue
                if deps.get(pk, 0) < pv:
                    deps[pk] = pv
            waits = []
            for k, v in deps.items():
                if self.waited[eng].get(k, 0) >= v:
                    continue
                self.waited[eng][k] = v
                waits.append((k, v))
            key = nd['key']
            if nd['inc'] == 16:
                self.dma_i[eng] = self.dma_i.get(eng, 0) + 1
                key = 'dq_%s_%d' % (eng, self.dma_i[eng] % NDMASEM)
                self.sem(key)
                if self.cnt[key] > 0 and self.waited[eng].get(key, 0) < self.cnt[key]:
                    self.waited[eng][key] = self.cnt[key]
                    waits.append((key, self.cnt[key]))
            self.sem(key)
            self.cnt[key] += nd['inc']
            semval[i] = (key, self.cnt[key])
            q[eng].append((waits, nd['fn'], key, nd['inc']))
        if self.drain_eng is not None:
            eng = self.drain_eng
            waits = []
            for k, v in self.cnt.items():
                if v > 0 and self.waited[eng].get(k, 0) < v and k != eng:
                    self.waited[eng][k] = v
                    waits.append((k, v))
            q[eng].append((waits, None, None, 0))
            self.drain_eng = None
        sems = self.sems

        def emit(e, lst):
            for waits, fn, key, inc in lst:
                for k, v in waits:
                    e.wait_ge(sems[k], v)
                if fn is not None:
                    fn(e).then_inc(sems[key], inc)

        with nc.Block() as block:
            @block.sync
            def _(e):
                emit(e, q['sp'])

            @block.scalar
            def _(e):
                emit(e, q['act'])

            @block.vector
            def _(e):
                emit(e, q['dve'])

            @block.gpsimd
            def _(e):
                emit(e, q['pool'])

            @block.tensor
            def _(e):
                emit(e, q['pe'])
        print("phase: %d nodes, sim %.0f us, busy us: %s" % (len(nodes), self.sim_ns / 1000.0,
                                                              ' '.join('%s=%.0f' % (e, v / 1000.0) for e, v in getattr(self, 'busy', {}).items())))
        self.nodes = []
        self.lastw = {}
        self.readers = {}


class RR:
    def __init__(self, engs):
        self.engs = engs
        self.i = 0

    def __call__(self):
        e = self.engs[self.i % len(self.engs)]
        self.i += 1
        return e


def copy_op(g, eng, out, in_, reads, writes):
    if eng == 'act':
        g.op('act', lambda e: e.copy(out=out, in_=in_), reads=reads, writes=writes)
    else:
        g.op(eng, lambda e: e.tensor_copy(out=out, in_=in_), reads=reads, writes=writes)


def make_consts(nc, g, es):
    c = {}
    sb = lambda name, shape, dt: es.enter_context(nc.sbuf_tensor(name, shape, dt))
    c['identf'] = sb("identf", [128, 128], F32)
    c['ident'] = sb("ident", [128, 128], BF16)
    c['Uf'] = sb("Uf", [128, 128], F32)
    c['Ub'] = sb("Ub", [128, 128], F32)
    c['onesf'] = sb("onesf", [128, 128], F32)
    c['onesb'] = sb("onesb", [128, 128], BF16)
    c['maskf'] = sb("maskf", [128, 512], BF16)
    c['maskb'] = sb("maskb", [128, 512], BF16)
    identf, ident, Uf, Ub, onesf, onesb, maskf, maskb = (c[k] for k in
                                                          ['identf', 'ident', 'Uf', 'Ub', 'onesf', 'onesb', 'maskf', 'maskb'])
    P = 'pool'
    g.op(P, lambda e: e.memset(onesf[:, :], 1.0), writes=['onesf'])
    g.op(P, lambda e: e.memset(onesb[:, :], 1.0), writes=['onesb'])
    g.op(P, lambda e: e.memset(identf[:, :], 1.0), writes=['identf'])
    g.op(P, lambda e: e.affine_select(out=identf[:, :], in_=identf[:, :], pattern=[[1, 128]], compare_op=ALU.is_equal,
                                      fill=0.0, base=0, channel_multiplier=-1), reads=['identf'], writes=['identf'])
    g.op(P, lambda e: e.tensor_copy(out=ident[:, :], in_=identf[:, :]), reads=['identf'], writes=['ident'])
    g.op(P, lambda e: e.memset(Uf[:, :], 1.0), writes=['Uf'])
    g.op(P, lambda e: e.affine_select(out=Uf[:, :], in_=Uf[:, :], pattern=[[1, 128]], compare_op=ALU.is_ge,
                                      fill=0.0, base=0, channel_multiplier=-1), reads=['Uf'], writes=['Uf'])
    g.op(P, lambda e: e.memset(Ub[:, :], 1.0), writes=['Ub'])
    g.op(P, lambda e: e.affine_select(out=Ub[:, :], in_=Ub[:, :], pattern=[[-1, 128]], compare_op=ALU.is_ge,
                                      fill=0.0, base=0, channel_multiplier=1), reads=['Ub'], writes=['Ub'])
    g.op(P, lambda e: e.memset(maskf[:, :], 0.0), writes=['maskf'])
    g.op(P, lambda e: e.memset(maskb[:, :], 0.0), writes=['maskb'])
    for h in range(4):
        sl = slice(h * 128, (h + 1) * 128)
        g.op(P, lambda e, sl=sl: e.affine_select(out=maskf[:, sl], in_=maskf[:, sl], pattern=[[1, 128]], compare_op=ALU.is_ge,
                                                 fill=NEG, base=0, channel_multiplier=-1), reads=['maskf'], writes=['maskf'])
        g.op(P, lambda e, sl=sl: e.affine_select(out=maskb[:, sl], in_=maskb[:, sl], pattern=[[-1, 128]], compare_op=ALU.is_ge,
                                                 fill=NEG, base=0, channel_multiplier=1), reads=['maskb'], writes=['maskb'])
    return c


WCH = {}


def load_weight_bf16(g, nc, es, W, wdram, r0, c0, c1, name, nk, stage, stage_keys, rr, chunk=1024):
    WCH[name] = chunk
    si = 0
    for cc in range(c0, c1, chunk):
        for kc in range(nk):
            ce = min(c1, cc + chunk)
            n = ce - cc
            st, sk = stage[si % len(stage)], stage_keys[si % len(stage)]
            si += 1
            g.dma(['sp', 'pool'][si % 2], 'wst' + sk, st[:, 0:n], wdram[r0 + kc * 128:r0 + (kc + 1) * 128, cc:ce], writes=[sk])
            copy_op(g, rr(), W[:, kc, cc - c0:ce - c0], st[:, 0:n], reads=[sk], writes=[(name, kc, (cc - c0) // chunk)])


def wk(name, kc, lo, hi):
    ch = WCH[name]
    return [(name, kc, b_) for b_ in range(lo // ch, (hi - 1) // ch + 1)]


def x_tile_T(g, nc, c, xdram, t0, nchunk, xin, xb, xT, ptr, keys, evac_eng):
    kin, kb, kT, kp = keys
    g.dma('sp', kin, xin[:, 0:nchunk, :], xdram[t0:t0 + 128 * nchunk, :].rearrange("(j p) d -> p j d", p=128), writes=[kin])
    g.op('act', lambda e: e.copy(out=xb[:, 0:nchunk, :], in_=xin[:, 0:nchunk, :]), reads=[kin], writes=[kb])
    for j in range(nchunk):
        for kc in range(8):
            g.op('pe', lambda e, j=j, kc=kc: e.transpose(out=ptr[:, kc, :], in_=xb[:, j, kc * 128:(kc + 1) * 128],
                                                       identity=c['ident'][:, :]), reads=[kb, 'ident'], writes=[kp])
        copy_op(g, evac_eng, xT[:, :, j * 128:(j + 1) * 128], ptr[:, :, :], reads=[kp], writes=[kT])


def phase_A1(nc, g, c, S, io, scr):
    T = 256
    NT = S // T
    with contextlib.ExitStack() as es:
        sb = lambda name, shape, dt: es.enter_context(nc.sbuf_tensor(name, shape, dt))
        ps = lambda name, shape, dt: es.enter_context(nc.psum_tensor(name, shape, dt))
        Wa = sb("Wa", [128, 8, 3136], BF16)
        wst = [sb("wst%d" % i, [128, 512], F32) for i in range(4)]
        cw6 = sb("cw6", [6, CONV], F32)
        wcol = sb("wcol", [128, 24, 8], F32)
        diag = sb("diag", [128, 24, 5, 128], BF16)
        dtb = sb("dtb", [128, 64], F32)
        Acoef = sb("Acoef", [128, 64], F32)
        uT = [sb("uT%d" % i, [128, 24, T + 4], BF16) for i in range(2)]
        xin = sb("xin", [128, 2, 1024], F32)
        xb = sb("xb", [128, 2, 1024], BF16)
        xT = [sb("xT%d" % i, [128, 8, T], BF16) for i in range(2)]
        xcst = [sb("xcst%d" % i, [128, 2560], BF16) for i in range(2)]
        bcst = [sb("bcst%d" % i, [128, 8, T], BF16) for i in range(2)]
        dtraw = sb("dtraw", [128, S // 128, 64], F32)
        dtw1 = sb("dtw1", [128, S // 512, 64], F32)
        dtout = sb("dtout", [128, S // 512, 128], F32)
        pin = [ps("pin%d" % i, [128, 512], F32) for i in range(2)]
        ptr = ps("ptr", [128, 8, 128], BF16)
        pdt = ps("pdt", [128, 512], F32)
        pcv = [ps("pcv%d" % i, [128, 512], F32) for i in range(2)]
        ptrc = [ps("ptrc%d" % i, [128, 8, 128], BF16) for i in range(2)]
        pmisc = pcv[0][:, 0:192].rearrange("p (a b) -> p a b", b=8)
        xfm = sb("xfm", [128, 16, T], BF16)

        rr = RR(['dve', 'act'])
        load_weight_bf16(g, nc, es, Wa, io['w_in'], 0, C_X, C_X + 3136, 'Wa', 8, wst, ['wst%d' % i for i in range(4)], rr, chunk=512)
        g.dma('pool', 'cw', cw6[1:6, :], io['conv_w'][:, :], writes=['cw6'])
        g.dma('pool', 'cw', cw6[0:1, :], io['conv_b'].rearrange("(o n) -> o n", o=1), writes=['cw6'])
        for ct in range(24):
            g.op('pe', lambda e, ct=ct: e.matmul(pmisc[:, ct, 0:6], lhsT=cw6[0:6, ct * 128:(ct + 1) * 128], rhs=c['identf'][0:6, 0:6],
                                                 start=True, stop=True), reads=['cw6', 'identf'], writes=['pcv0'])
        g.op('dve', lambda e: e.tensor_copy(out=wcol[:, :, :], in_=pmisc[:, :, :]), reads=['pcv0'], writes=['wcol'])
        for ct in range(24):
            for k in range(5):
                g.op(['dve', 'pool'][(ct * 5 + k) % 2],
                     lambda e, ct=ct, k=k: e.tensor_scalar_mul(out=diag[:, ct, k, :], in0=c['identf'][:, :], scalar1=wcol[:, ct, k + 1:k + 2]),
                     reads=['wcol', 'identf'], writes=['diag'])
        g.dma('pool', 'cw', dtb[:, 0:32], io['dt_bias_f'].rearrange("(o n) -> o n", o=1).broadcast_to([128, 32]), writes=['dtb'])
        g.dma('pool', 'cw', dtb[:, 32:64], io['dt_bias_b'].rearrange("(o n) -> o n", o=1).broadcast_to([128, 32]), writes=['dtb'])
        g.dma('pool', 'cw', Acoef[:, 0:32], io['a_log_f'].rearrange("(o n) -> o n", o=1).broadcast_to([128, 32]), writes=['Acoef'])
        g.dma('pool', 'cw', Acoef[:, 32:64], io['a_log_b'].rearrange("(o n) -> o n", o=1).broadcast_to([128, 32]), writes=['Acoef'])
        g.op('act', lambda e: e.activation(out=Acoef[:, :], in_=Acoef[:, :], func=AF.Exp), reads=['Acoef'], writes=['Acoef'])
        g.op('dve', lambda e: e.tensor_scalar_mul(out=Acoef[:, :], in0=Acoef[:, :], scalar1=-1.0), reads=['Acoef'], writes=['Acoef'])

        allct = lambda s: [('uT', s, ct) for ct in range(24)]

        def inproj(i):
            s = i % 2
            xs = i % 2
            x_tile_T(g, nc, c, io['x'], i * T, 2, xin, xb, xT[xs], ptr, ('xin', 'xb', 'xT%d' % xs, 'ptr'), 'dve')
            g.dma('pool', 'SxT', scr['xTs'].rearrange("(k p) t -> p k t", p=128)[:, :, i * T:(i + 1) * T], xT[xs][:, :, :],
                  reads=['xT%d' % xs], writes=[('scr_xTs', i)])
            for cp in range(12):
                pb = pin[cp % 2]
                for half in range(2):
                    ct = cp * 2 + half
                    for kc in range(8):
                        g.op('pe', lambda e, pb=pb, half=half, ct=ct, kc=kc, xs=xs: e.matmul(
                            pb[:, half * T:(half + 1) * T], lhsT=Wa[:, kc, ct * 128:(ct + 1) * 128], rhs=xT[xs][:, kc, :],
                            start=(kc == 0), stop=(kc == 7)), reads=wk('Wa', kc, ct * 128, ct * 128 + 128) + ['xT%d' % xs], writes=['pin%d' % (cp % 2)])
                copy_op(g, ['dve', 'act'][cp % 2], uT[s][:, cp * 2:cp * 2 + 2, 2:2 + T],
                        pb[:, :].rearrange("p (a t) -> p a t", a=2),
                        reads=['pin%d' % (cp % 2)], writes=[('uT', s, cp * 2), ('uT', s, cp * 2 + 1)])
            for j in range(2):
                for kc in range(8):
                    g.op('pe', lambda e, j=j, kc=kc, xs=xs: e.matmul(pdt[:, j * 64:(j + 1) * 64], lhsT=xT[xs][:, kc, j * 128:(j + 1) * 128],
                                                                    rhs=Wa[:, kc, 3072:3136], start=(kc == 0), stop=(kc == 7)),
                         reads=wk('Wa', kc, 3072, 3136) + ['xT%d' % xs], writes=['pdt'])
            g.op('dve', lambda e, i=i: e.tensor_copy(out=dtraw[:, i * 2:i * 2 + 2, :], in_=pdt[:, 0:128].rearrange("p (j n) -> p j n", j=2)),
                 reads=['pdt'], writes=['dtraw'])
            if i == 0:
                g.op('pool', lambda e, s=s: e.memset(uT[s][:, :, 0:2], 0.0), writes=allct(s))
            else:
                sp = (i - 1) % 2
                g.op('pool', lambda e, s=s, sp=sp: e.tensor_copy(out=uT[s][:, :, 0:2], in_=uT[sp][:, :, T:T + 2]),
                     reads=allct(sp), writes=allct(s))
                g.op('pool', lambda e, s=s, sp=sp: e.tensor_copy(out=uT[sp][:, :, T + 2:T + 4], in_=uT[s][:, :, 2:4]),
                     reads=allct(s), writes=allct(sp))
            if i == NT - 1:
                g.op('pool', lambda e, s=s: e.memset(uT[s][:, :, T + 2:T + 4], 0.0), writes=allct(s))

        def conv(i):
            s = i % 2
            bs = bcst[i % 2]
            bk = 'bcst%d' % (i % 2)
            for cp in range(12):
                pb = pcv[cp % 2]
                pk = 'pcv%d' % (cp % 2)
                for half in range(2):
                    ct = cp * 2 + half
                    for k in range(5):
                        g.op('pe', lambda e, pb=pb, half=half, ct=ct, k=k, s=s: e.matmul(
                            pb[:, half * T:(half + 1) * T], lhsT=diag[:, ct, k, :], rhs=uT[s][:, ct, k:k + T],
                            start=(k == 0), stop=(k == 4)), reads=[('uT', s, ct), 'diag'], writes=[pk])
                for half in range(2):
                    ct = cp * 2 + half
                    if ct < 16:
                        g.op('act', lambda e, pb=pb, half=half, ct=ct: e.activation(
                            out=xfm[:, ct, :], in_=pb[:, half * T:(half + 1) * T], func=AF.Silu, bias=wcol[:, ct, 0:1]),
                             reads=[pk, 'wcol'], writes=[('xfm', ct)])
                    else:
                        g.op('act', lambda e, pb=pb, half=half, ct=ct, bs=bs: e.activation(
                            out=bs[:, ct - 16, :], in_=pb[:, half * T:(half + 1) * T], func=AF.Silu, bias=wcol[:, ct, 0:1]),
                             reads=[pk, 'wcol'], writes=[(bk, ct - 16)])
            g.dma('pool', bk, scr['bct'].rearrange("(c p) t -> p c t", p=128)[:, :, i * T:(i + 1) * T], bs[:, :, :],
                  reads=[(bk, q_) for q_ in range(8)], writes=['scr_bct'])
            nt_ = 0
            for j in range(2):
                st = xcst[j]
                for grp, cts in enumerate([range(0, 8), range(8, 16), range(16, 20)]):
                    pt, ptk = ptrc[nt_ % 2], 'ptrc%d' % (nt_ % 2)
                    nt_ += 1
                    for q_, ct in enumerate(cts):
                        if ct < 16:
                            src, rk_ = xfm[:, ct, j * 128:(j + 1) * 128], ('xfm', ct)
                        else:
                            src, rk_ = bs[:, ct - 16, j * 128:(j + 1) * 128], (bk, ct - 16)
                        g.op('pe', lambda e, pt=pt, q_=q_, src=src: e.transpose(out=pt[:, q_, :], in_=src, identity=c['ident'][:, :]),
                             reads=[rk_, 'ident'], writes=[ptk])
                    n_ = len(cts)
                    copy_op(g, ['dve', 'act'][grp % 2], st[:, grp * 1024:grp * 1024 + n_ * 128].rearrange("p (a t) -> p a t", a=n_),
                            pt[:, 0:n_, :], reads=[ptk], writes=['xcst%d' % j])
                t0 = i * T + j * 128
                g.dma('pool', 'xcst%d' % j, scr['xc'][t0:t0 + 128, :], st[:, 0:2048], reads=['xcst%d' % j], writes=['scr_xc'])
                g.dma('pool', 'xcst%d' % j, scr['btok'][t0:t0 + 128, :], st[:, 2048:2560], reads=['xcst%d' % j], writes=['scr_btok'])

        for i in range(NT + 1):
            if i < NT:
                inproj(i)
            if i >= 1:
                conv(i - 1)

        NC_ = S // 128
        NB = 4
        CB_ = NC_ // NB
        bb = lambda t: t[:, :].unsqueeze(1).broadcast_to([128, CB_, 64])
        dsts = scr['dts'].rearrange("(j p) n -> p j n", p=128)
        for blk in range(NB):
            dr = dtraw[:, blk * CB_:(blk + 1) * CB_, :]
            g.op('dve', lambda e, dr=dr: e.tensor_tensor(out=dr, in0=dr, in1=bb(dtb), op=ALU.add), reads=['dtraw', 'dtb'], writes=['dtraw'])
            g.op('act', lambda e, dr=dr: e.activation(out=dtw1[:, :, :], in_=dr, func=AF.Abs), reads=['dtraw'], writes=['dtw1'])
            g.op('act', lambda e: e.activation(out=dtw1[:, :, :], in_=dtw1[:, :, :], func=AF.Exp, scale=-1.0), reads=['dtw1'], writes=['dtw1'])
            g.op('act', lambda e: e.activation(out=dtw1[:, :, :], in_=dtw1[:, :, :], func=AF.Ln, bias=1.0), reads=['dtw1'], writes=['dtw1'])
            g.op('dve', lambda e, dr=dr: e.scalar_tensor_tensor(out=dtout[:, :, 0:64], in0=dr, scalar=0.0, in1=dtw1[:, :, :],
                                                                op0=ALU.max, op1=ALU.add), reads=['dtraw', 'dtw1'], writes=['dtout'])
            g.op('dve', lambda e: e.tensor_tensor(out=dtout[:, :, 64:128], in0=dtout[:, :, 0:64], in1=bb(Acoef), op=ALU.mult),
                 reads=['dtout', 'Acoef'], writes=['dtout'])
            g.dma('sp', 'dtout', dsts[:, blk * CB_:(blk + 1) * CB_, :], dtout[:, :, :], reads=['dtout'], writes=['scr_dts'])
        g.drain('sp')
        g.flush()


def phase_A2(nc, g, c, S, io, scr):
    T = 512
    NT = S // T
    with contextlib.ExitStack() as es:
        sb = lambda name, shape, dt: es.enter_context(nc.sbuf_tensor(name, shape, dt))
        ps = lambda name, shape, dt: es.enter_context(nc.psum_tensor(name, shape, dt))
        Wq = sb("Wq", [128, 8, 2304], BF16)
        wst = [sb("wstq%d" % i, [128, 576], F32) for i in range(4)]
        xin = sb("xin2", [128, 4, 1024], F32)
        xb = sb("xb2", [128, 4, 1024], BF16)
        xT = [sb("xT2%d" % i, [128, 8, T], BF16) for i in range(2)]
        qst = [sb("qst%d" % i, [128, 12, T], BF16) for i in range(2)]
        vst = [sb("vst%d" % i, [128, AW], BF16) for i in range(2)]
        pin = [ps("pq%d" % i, [128, 512], F32) for i in range(3)]
        ptr = ps("ptr2", [128, 8, 128], BF16)
        pv = [ps("pv%d" % i, [128, 512], F32) for i in range(4)]
        rr = RR(['dve', 'act'])
        load_weight_bf16(g, nc, es, Wq, io['w_in'], 0, C_Q, C_Q + 2304, 'Wq', 8, wst, ['wstq%d' % i for i in range(4)], rr, chunk=576)
        for i in range(NT):
            xs = i % 2
            g.dma(['sp', 'pool'][i % 2], 'LxT2', xT[xs][:, :, :], scr['xTs'].rearrange("(k p) t -> p k t", p=128)[:, :, i * T:(i + 1) * T],
                  writes=['xT2%d' % xs])
            qs, qk = qst[i % 2], 'qst%d' % (i % 2)
            for ct in range(12):
                pb, pk = pin[ct % 3], 'pq%d' % (ct % 3)
                for kc in range(8):
                    g.op('pe', lambda e, pb=pb, ct=ct, kc=kc, xs=xs: e.matmul(pb[:, :], lhsT=Wq[:, kc, ct * 128:(ct + 1) * 128],
                                                                            rhs=xT[xs][:, kc, :], start=(kc == 0), stop=(kc == 7)),
                         reads=wk('Wq', kc, ct * 128, ct * 128 + 128) + ['xT2%d' % xs], writes=[pk])
                sc = 0.125 if ct < 6 else 1.0
                if ct % 2 == 0:
                    g.op('act', lambda e, pb=pb, ct=ct, qs=qs, sc=sc: e.mul(out=qs[:, ct, :], in_=pb[:, :], mul=sc), reads=[pk], writes=[qk])
                else:
                    g.op('dve', lambda e, pb=pb, ct=ct, qs=qs, sc=sc: e.tensor_scalar_mul(out=qs[:, ct, :], in0=pb[:, :], scalar1=sc),
                         reads=[pk], writes=[qk])
            g.dma('pool', qk, scr['qkT'].rearrange("(c p) t -> p c t", p=128)[:, :, i * T:(i + 1) * T], qs[:, :, :],
                  reads=[qk], writes=['scr_qkT'])
            for j in range(4):
                vs, vk = vst[j % 2], 'vst%d' % (j % 2)
                for hb, (c0, n) in enumerate([(0, 512), (512, 256)]):
                    pb, pk = pv[(j * 2 + hb) % 4], 'pv%d' % ((j * 2 + hb) % 4)
                    for kc in range(8):
                        g.op('pe', lambda e, pb=pb, kc=kc, xs=xs, j=j, c0=c0, n=n: e.matmul(
                            pb[:, 0:n], lhsT=xT[xs][:, kc, j * 128:(j + 1) * 128], rhs=Wq[:, kc, 1536 + c0:1536 + c0 + n],
                            start=(kc == 0), stop=(kc == 7)), reads=wk('Wq', kc, 1536 + c0, 1536 + c0 + n) + ['xT2%d' % xs], writes=[pk])
                    copy_op(g, ['act', 'dve'][hb], vs[:, c0:c0 + n], pb[:, 0:n], reads=[pk], writes=[vk])
                t0 = i * T + j * 128
                g.dma('pool', vk, scr['v'][t0:t0 + 128, :], vs[:, :], reads=[vk], writes=['scr_v'])
        g.drain('sp')
        g.flush()


def phase_SSD(nc, g, c, S, io, scr, direction):
    NCH = S // 128
    fw = direction == 0
    U = c['Uf'] if fw else c['Ub']
    Uk = 'Uf' if fw else 'Ub'
    mask = c['maskf'] if fw else c['maskb']
    mk = 'maskf' if fw else 'maskb'
    yout = scr['yf'] if fw else scr['yb']
    dcol = 0 if fw else 32
    order = list(range(NCH)) if fw else list(range(NCH - 1, -1, -1))
    with contextlib.ExitStack() as es:
        sb = lambda name, shape, dt: es.enter_context(nc.sbuf_tensor(name, shape, dt))
        ps = lambda name, shape, dt: es.enter_context(nc.psum_tensor(name, shape, dt))
        NSL = 3
        n2 = lambda base, shape, dt: [sb("%s%d_%d" % (base, direction, i), shape, dt) for i in range(NSL)]
        xc = n2("s_xc", [128, DIN], BF16)
        btok = n2("s_btok", [128, 512], BF16)
        bct = n2("s_bct", [128, 8, 128], BF16)
        dts = n2("s_dts", [128, 128], F32)
        negcs = n2("s_negcs", [128, 32], F32)
        ecs = n2("s_ecs", [128, 32], F32)
        dec = n2("s_dec", [128, 32], F32)
        cdec = n2("s_cdec", [128, 32], F32)
        xdt = n2("s_xdt", [128, DIN], BF16)
        xdd = n2("s_xdd", [128, DIN], BF16)
        LT = n2("s_LT", [128, 4, 128], BF16)
        MT = n2("s_MT", [128, 4, 128], BF16)
        CBT = n2("s_CBT", [128, 4, 128], BF16)
        t1 = n2("s_t1", [128, 512], F32)
        yst = n2("s_yst", [128, DIN], BF16)
        hst = sb("s_h%d" % direction, [128, DIN], F32)
        hbf = sb("s_hb%d" % direction, [128, DIN], BF16)
        htmp = sb("s_ht%d" % direction, [128, 512], F32)
        pdiff = [ps("pdiff%d_%d" % (direction, i), [128, 512], F32) for i in range(3)]
        pcb = ps("pcb%d" % direction, [128, 512], F32)
        pms = ps("pms%d" % direction, [128, 512], F32)
        pY = ps("pY%d" % direction, [128, 512], F32)
        pZ = ps("pZ%d" % direction, [128, 512], F32)
        pS = ps("pS%d" % direction, [128, 512], F32)
        tg = 's%d_' % direction
        g.op('pool', lambda e: e.memset(hst[:, :], 0.0), writes=[tg + 'h%d' % gg for gg in range(4)])
        g.op('pool', lambda e: e.memset(hbf[:, :], 0.0), writes=[tg + 'hb%d' % gg for gg in range(4)])
        ndc = [0]

        def prologue(it):
            ch = order[it]
            b = it % NSL
            K_ = lambda nm: tg + nm + str(b)
            t0 = ch * 128
            g.dma('sp', 'Lxc%d' % b, xc[b][:, :], scr['xc'][t0:t0 + 128, :], reads=['scr_xc'], writes=[K_('xc')])
            g.dma('sp', 'Lbtok%d' % b, btok[b][:, :], scr['btok'][t0:t0 + 128, :], reads=['scr_btok'], writes=[K_('btok')])
            g.dma('sp', 'Lbct%d' % b, bct[b][:, :, :], scr['bct'].rearrange("(c p) t -> p c t", p=128)[:, :, t0:t0 + 128],
                  reads=['scr_bct'], writes=[K_('bct')])
            g.dma('sp', 'Ldts%d' % b, dts[b][:, :], scr['dts'][t0:t0 + 128, :], reads=['scr_dts'], writes=[K_('dts')])
            a_ap = dts[b][:, 64 + dcol:64 + dcol + 32]
            dt_ap = dts[b][:, dcol:dcol + 32]
            g.op('pe', lambda e, a_ap=a_ap: e.matmul(pms[:, 0:32], lhsT=U[:, :], rhs=a_ap, start=True, stop=True),
                 reads=[Uk, K_('dts')], writes=[tg + 'pms'])
            g.op('pe', lambda e, a_ap=a_ap: e.matmul(pms[:, 32:64], lhsT=c['onesf'][:, :], rhs=a_ap, start=True, stop=True),
                 reads=['onesf', K_('dts')], writes=[tg + 'pms'])
            g.op('dve', lambda e, b=b: e.tensor_scalar_mul(out=negcs[b][:, :], in0=pms[:, 0:32], scalar1=-1.0),
                 reads=[tg + 'pms'], writes=[K_('negcs')])
            g.op('act', lambda e, b=b: e.activation(out=ecs[b][:, :], in_=pms[:, 0:32], func=AF.Exp), reads=[tg + 'pms'], writes=[K_('ecs')])
            g.op('act', lambda e, b=b: e.activation(out=cdec[b][:, :], in_=pms[:, 32:64], func=AF.Exp), reads=[tg + 'pms'], writes=[K_('cdec')])
            g.op('dve', lambda e, b=b: e.tensor_tensor(out=dec[b][:, :], in0=pms[:, 32:64], in1=negcs[b][:, :], op=ALU.add),
                 reads=[tg + 'pms', K_('negcs')], writes=[K_('dec')])
            g.op('act', lambda e, b=b: e.activation(out=dec[b][:, :], in_=dec[b][:, :], func=AF.Exp), reads=[K_('dec')], writes=[K_('dec')])
            g.op('pool', lambda e, b=b, dt_ap=dt_ap: e.tensor_tensor(
                out=xdt[b][:, :].rearrange("p (h q) -> p h q", q=HP), in0=xc[b][:, :].rearrange("p (h q) -> p h q", q=HP),
                in1=dt_ap.unsqueeze(2).broadcast_to([128, NH, HP]), op=ALU.mult), reads=[K_('xc'), K_('dts')], writes=[K_('xdt')])
            g.op('pool', lambda e, b=b: e.tensor_tensor(
                out=xdd[b][:, :].rearrange("p (h q) -> p h q", q=HP), in0=xdt[b][:, :].rearrange("p (h q) -> p h q", q=HP),
                in1=dec[b][:, :].unsqueeze(2).broadcast_to([128, NH, HP]), op=ALU.mult), reads=[K_('xdt'), K_('dec')], writes=[K_('xdd')])

        def mainpart(it):
            ch = order[it]
            b = it % NSL
            K_ = lambda nm: tg + nm + str(b)
            t0 = ch * 128
            nd = ndc[0]
            for gg in range(4):
                g.op('pe', lambda e, b=b, gg=gg: e.matmul(pcb[:, gg * 128:(gg + 1) * 128], lhsT=bct[b][:, gg, :], rhs=bct[b][:, 4 + gg, :],
                                                        start=True, stop=True), reads=[K_('bct')], writes=[tg + 'pcb'])
            copy_op(g, 'act', CBT[b][:, :, :], pcb[:, :].rearrange("p (a l) -> p a l", a=4), reads=[tg + 'pcb'], writes=[K_('CBT')])
            for gg in range(4):
                for hq in range(2):
                    pd = pdiff[nd % 3]
                    pdk = tg + 'pdiff%d' % (nd % 3)
                    lt, ltk = LT[nd % 3], tg + 'LT%d' % (nd % 3)
                    mt, mtk = MT[nd % 3], tg + 'MT%d' % (nd % 3)
                    nd += 1
                    ndc[0] = nd
                    g.op('pe', lambda e, pd=pd: e.matmul(pd[:, :], lhsT=c['ident'][:, :], rhs=mask[:, :], start=True, stop=False),
                         reads=['ident', mk], writes=[pdk])
                    for hh in range(4):
                        h = gg * 8 + hq * 4 + hh
                        g.op('pe', lambda e, pd=pd, hh=hh, h=h, b=b: e.matmul(
                            pd[:, hh * 128:(hh + 1) * 128], lhsT=dts[b][:, 64 + dcol + h:64 + dcol + h + 1].broadcast_to([128, 128]),
                            rhs=U[:, :], start=False, stop=(hh == 3)), reads=[K_('dts'), Uk], writes=[pdk])
                    for hh in range(4):
                        h = gg * 8 + hq * 4 + hh
                        g.op('act', lambda e, pd=pd, hh=hh, h=h, b=b, lt=lt: e.activation(
                            out=lt[:, hh, :], in_=pd[:, hh * 128:(hh + 1) * 128], func=AF.Exp, bias=negcs[b][:, h:h + 1], scale=1.0),
                             reads=[pdk, K_('negcs')], writes=[ltk])
                    g.op('dve', lambda e, lt=lt, mt=mt, b=b, gg=gg: e.tensor_tensor(
                        out=mt[:, :, :], in0=lt[:, :, :], in1=CBT[b][:, gg:gg + 1, :].broadcast_to([128, 4, 128]), op=ALU.mult),
                         reads=[ltk, K_('CBT')], writes=[mtk])
                    for hh in range(4):
                        h = gg * 8 + hq * 4 + hh
                        hl = hq * 4 + hh
                        g.op('pe', lambda e, mt=mt, hh=hh, h=h, hl=hl, b=b: e.matmul(
                            pY[:, hl * HP:(hl + 1) * HP], lhsT=mt[:, hh, :], rhs=xdt[b][:, h * HP:(h + 1) * HP], start=True, stop=True),
                             reads=[mtk, K_('xdt')], writes=[tg + 'pY'])
                gs = slice(gg * 512, (gg + 1) * 512)
                g.op('pe', lambda e, b=b, gg=gg, gs=gs: e.matmul(pZ[:, :], lhsT=bct[b][:, 4 + gg, :], rhs=hbf[:, gs], start=True, stop=True),
                     reads=[K_('bct'), tg + 'hb%d' % gg], writes=[tg + 'pZ'])
                tt, ttk = t1[gg % 2], tg + 't1%d' % (gg % 2)
                g.op('dve', lambda e, tt=tt, b=b, gg=gg: e.tensor_tensor(
                    out=tt[:, :].rearrange("p (h q) -> p h q", q=HP), in0=pZ[:, :].rearrange("p (h q) -> p h q", q=HP),
                    in1=ecs[b][:, gg * 8:(gg + 1) * 8].unsqueeze(2).broadcast_to([128, 8, HP]), op=ALU.mult),
                     reads=[tg + 'pZ', K_('ecs')], writes=[ttk])
                g.op('dve', lambda e, tt=tt, b=b, gs=gs: e.tensor_tensor(out=yst[b][:, gs], in0=pY[:, :], in1=tt[:, :], op=ALU.add),
                     reads=[tg + 'pY', ttk], writes=[K_('yst')])
                g.op('pe', lambda e, b=b, gg=gg, gs=gs: e.matmul(pS[:, :], lhsT=btok[b][:, gg * 128:(gg + 1) * 128], rhs=xdd[b][:, gs],
                                                               start=True, stop=True), reads=[K_('btok'), K_('xdd')], writes=[tg + 'pS'])
                g.op('pool', lambda e, b=b, gg=gg, gs=gs: e.tensor_tensor(
                    out=htmp[:, :].rearrange("p (h q) -> p h q", q=HP), in0=hst[:, gs].rearrange("p (h q) -> p h q", q=HP),
                    in1=cdec[b][:, gg * 8:(gg + 1) * 8].unsqueeze(2).broadcast_to([128, 8, HP]), op=ALU.mult),
                     reads=[tg + 'h%d' % gg, K_('cdec')], writes=[tg + 'htmp'])
                g.op('dve', lambda e, gs=gs: e.tensor_tensor(out=hst[:, gs], in0=pS[:, :], in1=htmp[:, :], op=ALU.add),
                     reads=[tg + 'pS', tg + 'htmp'], writes=[tg + 'h%d' % gg])
                copy_op(g, 'act', hbf[:, gs], hst[:, gs], reads=[tg + 'h%d' % gg], writes=[tg + 'hb%d' % gg])
            g.dma('pool', 'Syst%d' % b, yout[t0:t0 + 128, :], yst[b][:, :], reads=[K_('yst')], writes=['scr_y%d' % direction])

        SKEW = int(os.environ.get('SSD_SKEW', '0'))
        if SKEW:
            prologue(0)
        for it in range(NCH):
            if SKEW:
                if it + 1 < NCH:
                    prologue(it + 1)
            else:
                prologue(it)
            mainpart(it)
        g.drain('sp')
        g.flush()


def phase_AT(nc, g, c, S, io, scr):
    slopes = [2.0 ** (-8.0 * (h + 1) / 12) for h in range(12)]
    VW = 72
    for gi, dil in enumerate([1, 4, 16]):
        Ls = S // dil
        NQ = Ls // 128
        NM = NQ + 1
        PADT = 64 * dil
        att = scr['att%d' % gi]
        with contextlib.ExitStack() as es:
            sb = lambda name, shape, dt: es.enter_context(nc.sbuf_tensor(name, shape, dt))
            ps = lambda name, shape, dt: es.enter_context(nc.psum_tensor(name, shape, dt))
            QT = sb("QT%d" % gi, [128, 2, S], BF16)
            KT = sb("KT%d" % gi, [128, 2, S + 2 * PADT], BF16)
            V = sb("V%d" % gi, [128, dil * NM, 4, VW], BF16)
            stage = sb("ast%d" % gi, [128, dil * NQ, 4, VW], F32)
            ALB = sb("ALB%d" % gi, [128, 4, 256], F32)
            R = sb("R%d" % gi, [128, 256], F32)
            sT = [sb("sT%d_%d" % (gi, i), [128, 256], F32) for i in range(2)]
            pT = [sb("pT%d_%d" % (gi, i), [128, 256], BF16) for i in range(2)]
            pS = [ps("pS%d_%d" % (gi, i), [128, 512], F32) for i in range(2)]
            pO = [ps("pO%d_%d" % (gi, i), [128, 512], F32) for i in range(2)]
            g.op('pool', lambda e: e.iota(R[:, 0:128], pattern=[[-1, 128]], base=-64, channel_multiplier=1,
                                          allow_small_or_imprecise_dtypes=True), writes=['R'])
            g.op('pool', lambda e: e.iota(R[:, 128:256], pattern=[[-1, 128]], base=64, channel_multiplier=1,
                                          allow_small_or_imprecise_dtypes=True), reads=['R'], writes=['R'])
            g.op('act', lambda e: e.activation(out=R[:, :], in_=R[:, :], func=AF.Abs), reads=['R'], writes=['R'])
            for hl in range(4):
                sl = -slopes[gi * 4 + hl] * dil
                g.op('dve', lambda e, hl=hl, sl=sl: e.tensor_scalar_mul(out=ALB[:, hl, :], in0=R[:, :], scalar1=sl), reads=['R'], writes=['ALB'])
                g.op('pool', lambda e, hl=hl: e.affine_select(out=ALB[:, hl, 0:128], in_=ALB[:, hl, 0:128], pattern=[[-1, 128]], compare_op=ALU.is_ge,
                                                              fill=NEG, base=0, channel_multiplier=1), reads=['ALB'], writes=['ALB'])
                g.op('pool', lambda e, hl=hl: e.affine_select(out=ALB[:, hl, 128:256], in_=ALB[:, hl, 128:256], pattern=[[1, 128]], compare_op=ALU.is_ge,
                                                              fill=NEG, base=0, channel_multiplier=-1), reads=['ALB'], writes=['ALB'])
            g.op('pool', lambda e: e.memset(KT[:, :, 0:PADT], 0.0), writes=['KT'])
            g.op('pool', lambda e: e.memset(KT[:, :, PADT + S:PADT + S + PADT], 0.0), writes=['KT'])
            for hp in range(2):
                r0 = gi * 256 + hp * 128
                NP_ = 4
                for pc in range(NP_):
                    c0, c1 = pc * S // NP_, (pc + 1) * S // NP_
                    g.dma(['sp', 'pool'][pc % 2], 'Lq', QT[:, hp, c0:c1], scr['qkT'][r0:r0 + 128, c0:c1], reads=['scr_qkT'], writes=[('QT', hp, pc)])
                    g.dma(['pool', 'sp'][pc % 2], 'Lk', KT[:, hp, PADT + c0:PADT + c1], scr['qkT'][768 + r0:768 + r0 + 128, c0:c1],
                          reads=['scr_qkT', 'KT'], writes=[('KT', hp, pc)])
            g.op('pool', lambda e: e.memset(V[:, :, :, 64:VW], 0.0), writes=['V'])
            g.op('pool', lambda e: e.memset(V[:, :, :, 64:65], 1.0), reads=['V'], writes=['V'])
            vsrc = scr['v'].rearrange("(j d) c -> d j c", d=dil)
            for r in range(dil):
                g.op('pool', lambda e, r=r: e.memset(V[0:64, r * NM, :, :], 0.0), reads=['V'], writes=['V'])
                g.op('pool', lambda e, r=r: e.memset(V[64:128, r * NM + NM - 1, :, :], 0.0), reads=['V'], writes=['V'])
            for r in range(dil):
                for m in range(NM):
                    lo = 128 * m - 64
                    p0, p1 = (64, 128) if m == 0 else ((0, 64) if m == NM - 1 else (0, 128))
                    i0 = lo + p0
                    g.dma(['sp', 'pool'][(r * NM + m) % 2], 'Lv%d' % ((r * NM + m) % 2), V[p0:p1, r * NM + m, :, 0:64],
                          vsrc[r, i0:i0 + (p1 - p0), gi * 256:(gi + 1) * 256].rearrange("j (h e) -> j h e", e=64),
                          reads=['scr_v', 'V'], writes=[('Vt', r * NM + m)])
            ns = 0
            for hl in range(4):
                hp, po = hl // 2, (hl % 2) * 64
                Kv = KT[po:po + 64, hp, :].rearrange("p (j d) -> p d j", d=dil)
                Qv = QT[po:po + 64, hp, :].rearrange("p (j d) -> p d j", d=dil)
                for r in range(dil):
                    for n in range(NQ):
                        psb, psk = pS[ns % 2], 'pS%d' % (ns % 2)
                        st_, stk = sT[ns % 2], 'sT%d' % (ns % 2)
                        pt_, ptk = pT[ns % 2], 'pT%d' % (ns % 2)
                        pob, pok = pO[ns % 2], 'pO%d' % (ns % 2)
                        ns += 1
                        for half in range(2):
                            m = n + half
                            g.op('pe', lambda e, psb=psb, Kv=Kv, Qv=Qv, r=r, m=m, n=n, half=half: e.matmul(
                                psb[:, half * 128:(half + 1) * 128], lhsT=Kv[:, r, 128 * m:128 * m + 128], rhs=Qv[:, r, 128 * n:128 * n + 128],
                                start=True, stop=True), reads=['KT'] + [('KT', hp, pc) for pc in range(4)] + [('QT', hp, pc) for pc in range(4)], writes=[psk])
                        g.op('dve', lambda e, psb=psb, st_=st_, hl=hl: e.tensor_tensor(
                            out=st_[:, :], in0=psb[:, 0:256], in1=ALB[:, hl, :], op=ALU.add), reads=[psk, 'ALB'], writes=[stk])
                        g.op('act', lambda e, st_=st_, pt_=pt_: e.activation(out=pt_[:, :], in_=st_[:, :], func=AF.Exp), reads=[stk], writes=[ptk])
                        for half in range(2):
                            vt = r * NM + n + half
                            g.op('pe', lambda e, pt_=pt_, pob=pob, vt=vt, hl=hl, half=half: e.matmul(
                                pob[:, 0:VW], lhsT=pt_[:, half * 128:(half + 1) * 128], rhs=V[:, vt, hl, :], start=(half == 0), stop=(half == 1)),
                                 reads=[ptk, 'V', ('Vt', vt)], writes=[pok])
                        copy_op(g, 'dve' if ns % 2 else 'act', stage[:, r * NQ + n, hl, :], pob[:, 0:VW],
                                reads=[pok], writes=[('ast', r * NQ + n)])
            adst = att.rearrange("(j d) c -> d j c", d=dil)
            for r in range(dil):
                for n in range(NQ):
                    g.dma(['sp', 'pool'][n % 2], 'Sat%d' % (n % 2), adst[r, 128 * n:128 * n + 128, :],
                          stage[:, r * NQ + n, :, :].rearrange("p h e -> p (h e)"), reads=[('ast', r * NQ + n)], writes=[('scr_att%d' % gi, r, n)])
            g.drain('sp')
            g.flush()


def layer_norm_ops(g, r, rk, gam, bet, out, ok, junk, jk, st, sk):
    for k in range(2):
        g.op('dve', lambda e, k=k: e.bn_stats(out=st[:, 6 * k:6 * k + 6], in_=r[:, k * 512:(k + 1) * 512]), reads=[rk], writes=[sk])
    g.op('dve', lambda e: e.bn_aggr(out=st[:, 12:14], in_=st[:, 0:12].rearrange("p (a b) -> p a b", b=6)), reads=[sk], writes=[sk])
    g.op('dve', lambda e: e.tensor_scalar_add(out=st[:, 14:15], in0=st[:, 13:14], scalar1=EPS), reads=[sk], writes=[sk])
    g.op('act', lambda e: e.activation(out=st[:, 15:16], in_=st[:, 14:15], func=AF.Sqrt), reads=[sk], writes=[sk])
    g.op('dve', lambda e: e.reciprocal(out=st[:, 16:17], in_=st[:, 15:16]), reads=[sk], writes=[sk])
    g.op('dve', lambda e: e.scalar_tensor_tensor(out=st[:, 17:18], in0=st[:, 12:13], scalar=-1.0, in1=st[:, 16:17], op0=ALU.mult, op1=ALU.mult),
         reads=[sk], writes=[sk])
    g.op('act', lambda e: e.activation(out=out[:, :], in_=r[:, :], func=AF.Identity, bias=st[:, 17:18], scale=st[:, 16:17]),
         reads=[rk, sk], writes=[ok])
    g.op('dve', lambda e: e.tensor_tensor(out=out[:, :], in0=out[:, :], in1=gam[:, :], op=ALU.mult), reads=[ok, 'lng'], writes=[ok])
    g.op('pool', lambda e: e.tensor_tensor(out=out[:, :], in0=out[:, :], in1=bet[:, :], op=ALU.add), reads=[ok, 'lnb'], writes=[ok])


def bcast_row(g, tile, key, src, n, q='pool'):
    g.dma(q, 'cw', tile[:, 0:n], src.rearrange("(o n) -> o n", o=1).broadcast_to([128, n]), writes=[key])


def phase_CZ(nc, g, c, S, io, scr):
    NCH = S // 128
    with contextlib.ExitStack() as es:
        sb = lambda name, shape, dt: es.enter_context(nc.sbuf_tensor(name, shape, dt))
        ps = lambda name, shape, dt: es.enter_context(nc.psum_tensor(name, shape, dt))
        NSL = int(os.environ.get("CZ_SLOTS", "3"))
        n2 = lambda base, shape, dt: [sb("%s_%d" % (base, i), shape, dt) for i in range(NSL)]
        Wz = sb("Wz", [128, 8, 2048], BF16)
        wst = [sb("wstz%d" % i, [128, 512], F32) for i in range(4)]
        Dsk = sb("z_Dsk", [128, 32], F32)
        xin = n2("z_xin", [128, 1, 1024], F32)
        xb = n2("z_xb", [128, 1, 1024], BF16)
        xT = n2("z_xT", [128, 8, 128], BF16)
        yf = n2("z_yf", [128, DIN], BF16)
        yb = n2("z_yb", [128, DIN], BF16)
        xc = n2("z_xc", [128, DIN], BF16)
        ys = n2("z_ys", [128, DIN], F32)
        tmp = n2("z_tmp", [128, DIN], BF16)
        yn = n2("z_yn", [128, DIN], BF16)
        ynT = n2("z_ynT", [128, 16, 128], BF16)
        sz = [sb("z_sz%d" % i, [128, 512], F32) for i in range(3)]
        junk = [sb("z_junk%d" % i, [128, 512], F32) for i in range(2)]
        ss = n2("z_ss", [128, 8], F32)
        ptr = [ps("z_ptr%d" % i, [128, 8, 128], BF16) for i in range(2)]
        pz = [ps("z_pz%d" % i, [128, 512], F32) for i in range(4)]
        py = [ps("z_py%d" % i, [128, 512], F32) for i in range(2)]
        rr = RR(['dve', 'act'])
        load_weight_bf16(g, nc, es, Wz, io['w_in'], 0, C_Z, C_Z + 2048, 'Wz', 8, wst, ['wstz%d' % i for i in range(4)], rr, chunk=512)
        bcast_row(g, Dsk, 'Dsk', io['d_skip'], 32)
        nz = 0
        for ch in range(NCH):
            t0 = ch * 128
            b = ch % NSL
            K_ = lambda nm: 'z_%s%d' % (nm, b)
            g.dma('sp', 'Lyf', yf[b][:, :], scr['yf'][t0:t0 + 128, :], writes=[K_('yf')])
            g.dma('sp', 'Lyb', yb[b][:, :], scr['yb'][t0:t0 + 128, :], writes=[K_('yb')])
            g.dma('sp', 'Lxc', xc[b][:, :], scr['xc'][t0:t0 + 128, :], writes=[K_('xc')])
            g.dma('sp', 'LxT', xT[b][:, :, :], scr['xTs'].rearrange("(k p) t -> p k t", p=128)[:, :, t0:t0 + 128], writes=[K_('xT')])
            g.op('pool', lambda e, b=b: e.tensor_tensor(out=tmp[b][:, :].rearrange("p (h q) -> p h q", q=HP), in0=xc[b][:, :].rearrange("p (h q) -> p h q", q=HP),
                                                      in1=Dsk[:, :].unsqueeze(2).broadcast_to([128, NH, HP]), op=ALU.mult),
                 reads=[K_('xc'), 'Dsk'], writes=[K_('tmp')])
            for gg in range(4):
                gs = slice(gg * 512, (gg + 1) * 512)
                pb, pk = pz[nz % 4], 'z_pz%d' % (nz % 4)
                szz, szk = sz[nz % 3], 'z_sz%d' % (nz % 3)
                jk_, jkk = junk[nz % 2], 'z_junk%d' % (nz % 2)
                nz += 1
                for kc in range(8):
                    g.op('pe', lambda e, pb=pb, kc=kc, gs=gs, b=b: e.matmul(pb[:, :], lhsT=xT[b][:, kc, :], rhs=Wz[:, kc, gs], start=(kc == 0), stop=(kc == 7)),
                         reads=[K_('xT')] + wk('Wz', kc, gg * 512, gg * 512 + 512), writes=[pk])
                g.op('act', lambda e, pb=pb, szz=szz: e.activation(out=szz[:, :], in_=pb[:, :], func=AF.Silu), reads=[pk], writes=[szk])
                py_, pyk = py[gg % 2], 'z_py%d' % (gg % 2)
                for q_, (src, sk_) in enumerate([(yf[b], K_('yf')), (yb[b], K_('yb')), (tmp[b], K_('tmp'))]):
                    g.op('pe', lambda e, py_=py_, src=src, gs=gs, q_=q_: e.matmul(py_[:, :], lhsT=c['ident'][:, :], rhs=src[:, gs], start=(q_ == 0), stop=(q_ == 2)),
                         reads=['ident', sk_], writes=[pyk])
                g.op('dve', lambda e, szz=szz, gs=gs, b=b, py_=py_: e.tensor_tensor(out=ys[b][:, gs], in0=py_[:, :], in1=szz[:, :], op=ALU.mult),
                     reads=[pyk, szk], writes=[K_('ys')])
                g.op('dve', lambda e, gs=gs, gg=gg, b=b, jk_=jk_: e.scalar_tensor_tensor(out=jk_[:, :], in0=ys[b][:, gs], scalar=1.0, in1=ys[b][:, gs],
                                                                                      op0=ALU.mult, op1=ALU.mult, accum_out=ss[b][:, gg:gg + 1]),
                     reads=[K_('ys')], writes=[jkk, K_('ss')])
            g.op('dve', lambda e, b=b: e.tensor_scalar(out=ss[b][:, 0:4], in0=ss[b][:, 0:4], scalar1=1.0 / 512, scalar2=EPS, op0=ALU.mult, op1=ALU.add),
                 reads=[K_('ss')], writes=[K_('ss')])
            g.op('act', lambda e, b=b: e.activation(out=ss[b][:, 0:4], in_=ss[b][:, 0:4], func=AF.Sqrt), reads=[K_('ss')], writes=[K_('ss')])
            g.op('dve', lambda e, b=b: e.reciprocal(out=ss[b][:, 4:8], in_=ss[b][:, 0:4]), reads=[K_('ss')], writes=[K_('ss')])
            for gg in range(4):
                gs = slice(gg * 512, (gg + 1) * 512)
                g.op('act', lambda e, gs=gs, gg=gg, b=b: e.mul(out=yn[b][:, gs], in_=ys[b][:, gs], mul=ss[b][:, 4 + gg:5 + gg]),
                     reads=[K_('ys'), K_('ss')], writes=[K_('yn')])
            for hh in range(2):
                pt_, ptk_ = ptr[(2 * ch + hh) % 2], 'z_ptr%d' % ((2 * ch + hh) % 2)
                for k8 in range(8):
                    kc = hh * 8 + k8
                    g.op('pe', lambda e, k8=k8, kc=kc, b=b, pt_=pt_: e.transpose(out=pt_[:, k8, :], in_=yn[b][:, kc * 128:(kc + 1) * 128], identity=c['ident'][:, :]),
                         reads=[K_('yn'), 'ident'], writes=[ptk_])
                copy_op(g, ['act', 'dve'][hh], ynT[b][:, hh * 8:(hh + 1) * 8, :], pt_[:, :, :], reads=[ptk_], writes=[K_('ynT')])
            g.dma('pool', 'SynT', scr['ynT'].rearrange("(k p) t -> p k t", p=128)[:, :, t0:t0 + 128], ynT[b][:, :, :],
                  reads=[K_('ynT')], writes=[('scr_ynT', ch)])
        g.drain('sp')
        g.flush()


def phase_C(nc, g, c, S, io, scr):
    NCH = S // 128
    with contextlib.ExitStack() as es:
        sb = lambda name, shape, dt: es.enter_context(nc.sbuf_tensor(name, shape, dt))
        ps = lambda name, shape, dt: es.enter_context(nc.psum_tensor(name, shape, dt))
        n2 = lambda base, shape, dt: [sb("%s_%d" % (base, i), shape, dt) for i in range(2)]
        Wg = sb("Wg", [128, 8, 2048], BF16)
        Wps = sb("Wps", [128, 16, 1024], BF16)
        Wo = sb("Wo", [128, 8, 1024], BF16)
        Wpa = sb("Wpa", [128, 2, 1024], BF16)
        wst = [sb("wstc%d" % i, [128, 512], F32) for i in range(4)]
        nwrow = sb("nwrow", [16, 128], F32)
        nwcol = sb("nwcol", [128, 16], F32)
        bgf = sb("bgf", [1, 2048], F32)
        bgb = sb("bgb", [1, 2048], BF16)
        lng = sb("lng1", [128, 1024], F32)
        lnb = sb("lnb1", [128, 1024], F32)
        xin = n2("c_xin", [128, 1, 1024], F32)
        xb = n2("c_xb", [128, 1, 1024], BF16)
        xT = n2("c_xT", [128, 8, 128], BF16)
        yn = n2("c_yn", [128, DIN], BF16)
        ynT = n2("c_ynT", [128, 16, 128], BF16)
        at = [n2("c_at%d" % i, [128, 4, 72], F32) for i in range(3)]
        ya = n2("c_ya", [128, 4, 64], BF16)
        yaT = n2("c_yaT", [128, 2, 128], BF16)
        sg = [sb("c_sg%d" % i, [128, 512], F32) for i in range(3)]
        mm = n2("c_m", [128, 1024], F32)
        mt2 = n2("c_m2", [128, 512], F32)
        mb = n2("c_mb", [128, 1024], BF16)
        mT = n2("c_mT", [128, 8, 128], BF16)
        rr_ = n2("c_r", [128, 1024], F32)
        h1 = n2("c_h1", [128, 1024], F32)
        st = n2("c_st", [128, 20], F32)
        ptr = [ps("c_ptr%d" % i, [128, 8, 128], BF16) for i in range(2)]
        pA = [ps("c_pA%d" % i, [128, 512], F32) for i in range(3)]
        pG = [ps("c_pG%d" % i, [128, 512], F32) for i in range(3)]
        rr = RR(['dve', 'act'])
        load_weight_bf16(g, nc, es, Wg, io['w_in'], 0, C_G, C_G + 2048, 'Wg', 8, wst, ['wstc%d' % i for i in range(4)], rr, chunk=512)
        load_weight_bf16(g, nc, es, Wo, io['w_out'], 0, 0, 1024, 'Wo', 8, wst, ['wstc%d' % i for i in range(4)], rr, chunk=512)
        load_weight_bf16(g, nc, es, Wpa, io['w_proj_attn'], 0, 0, 1024, 'Wpa', 2, wst, ['wstc%d' % i for i in range(4)], rr, chunk=512)
        g.dma('pool', 'cw', nwrow[:, :], io['ssd_norm_w'].rearrange("(k p) -> k p", p=128), writes=['nwrow'])
        g.op('pe', lambda e: e.matmul(pA[0][:, 0:16], lhsT=nwrow[0:16, :], rhs=c['identf'][0:16, 0:16], start=True, stop=True),
             reads=['nwrow', 'identf'], writes=['c_pA0'])
        g.op('dve', lambda e: e.tensor_copy(out=nwcol[:, :], in_=pA[0][:, 0:16]), reads=['c_pA0'], writes=['nwcol'])
        for kc in range(16):
            for hf in range(2):
                wi = (kc * 2 + hf) % 4
                sk = 'wstc%d' % wi
                g.dma(['sp', 'pool'][wi % 2], 'wst' + sk, wst[wi][:, :], io['w_proj_ssd'][kc * 128:(kc + 1) * 128, hf * 512:(hf + 1) * 512], writes=[sk])
                g.op('dve', lambda e, kc=kc, hf=hf, wi=wi: e.tensor_scalar_mul(out=Wps[:, kc, hf * 512:(hf + 1) * 512], in0=wst[wi][:, :],
                                                                                    scalar1=nwcol[:, kc:kc + 1]), reads=[sk, 'nwcol'], writes=['Wps'])
        bcast_row(g, lng, 'lng', io['ln1_g'], 1024)
        bcast_row(g, lnb, 'lnb', io['ln1_b'], 1024)
        g.dma('pool', 'cw', bgf[0:1, :], io['b_gate'].rearrange("(o n) -> o n", o=1), writes=['bgf'])
        g.op('dve', lambda e: e.tensor_copy(out=bgb[:, :], in_=bgf[:, :]), reads=['bgf'], writes=['bgb'])
        na = 0
        ng = 0
        for ch in range(NCH):
            t0 = ch * 128
            b = ch % 2
            K_ = lambda nm: 'c_%s%d' % (nm, b)
            g.dma('sp', 'LynT', ynT[b][:, :, :], scr['ynT'].rearrange("(k p) t -> p k t", p=128)[:, :, t0:t0 + 128], writes=[K_('ynT')])
            for i in range(3):
                g.dma('sp', 'Lat%d' % i, at[i][b][:, :, :], scr['att%d' % i][t0:t0 + 128, :].rearrange("p (h e) -> p h e", e=72),
                      writes=[K_('at%d' % i)])
            g.dma('sp', 'LxT', xT[b][:, :, :], scr['xTs'].rearrange("(k p) t -> p k t", p=128)[:, :, t0:t0 + 128], writes=[K_('xT')])
            g.dma('pool', 'Lxin', xin[b][:, 0, :], io['x'][t0:t0 + 128, :], writes=[K_('xin')])
            for half in range(2):
                cs_ = slice(half * 512, (half + 1) * 512)
                pa, pak = pA[na % 3], 'c_pA%d' % (na % 3)
                na += 1
                pg, pgk = pG[ng % 3], 'c_pG%d' % (ng % 3)
                sgt, sgk = sg[ng % 3], 'c_sg%d' % (ng % 3)
                ng += 1
                for kc in range(16):
                    g.op('pe', lambda e, pa=pa, kc=kc, cs_=cs_, b=b: e.matmul(pa[:, :], lhsT=ynT[b][:, kc, :], rhs=Wps[:, kc, cs_],
                                                                            start=(kc == 0), stop=(kc == 15)), reads=[K_('ynT'), 'Wps'], writes=[pak])
                for kc in range(8):
                    g.op('pe', lambda e, pg=pg, kc=kc, cs_=cs_, b=b: e.matmul(pg[:, :], lhsT=xT[b][:, kc, :], rhs=Wg[:, kc, cs_],
                                                                            start=(kc == 0), stop=False), reads=[K_('xT')] + wk('Wg', kc, half * 512, half * 512 + 512), writes=[pgk])
                g.op('pe', lambda e, pg=pg, cs_=cs_: e.matmul(pg[:, :], lhsT=c['onesb'][0:1, 0:128], rhs=bgb[0:1, cs_], start=False, stop=True),
                     reads=['onesb', 'bgb'], writes=[pgk])
                g.op('act', lambda e, pg=pg, sgt=sgt: e.activation(out=sgt[:, :], in_=pg[:, :], func=AF.Sigmoid), reads=[pgk], writes=[sgk])
                g.op('dve', lambda e, pa=pa, sgt=sgt, cs_=cs_, b=b: e.tensor_tensor(out=mm[b][:, cs_], in0=pa[:, :], in1=sgt[:, :], op=ALU.mult),
                     reads=[pak, sgk], writes=[K_('m')])
            g.op('pool', lambda e, b=b: e.tensor_tensor(out=at[0][b][:, :, 0:65], in0=at[0][b][:, :, 0:65], in1=at[1][b][:, :, 0:65], op=ALU.add),
                 reads=[K_('at0'), K_('at1')], writes=[K_('at0')])
            g.op('pool', lambda e, b=b: e.tensor_tensor(out=at[0][b][:, :, 0:65], in0=at[0][b][:, :, 0:65], in1=at[2][b][:, :, 0:65], op=ALU.add),
                 reads=[K_('at0'), K_('at2')], writes=[K_('at0')])
            g.op('dve', lambda e, b=b: e.reciprocal(out=at[1][b][:, :, 64:65], in_=at[0][b][:, :, 64:65]), reads=[K_('at0'), K_('at1')], writes=[K_('at1')])
            g.op('dve', lambda e, b=b: e.tensor_tensor(out=ya[b][:, :, :], in0=at[0][b][:, :, 0:64], in1=at[1][b][:, :, 64:65].broadcast_to([128, 4, 64]), op=ALU.mult),
                 reads=[K_('at0'), K_('at1')], writes=[K_('ya')])
            for k2 in range(2):
                g.op('pe', lambda e, k2=k2, b=b: e.transpose(out=ptr[b][:, k2, :], in_=ya[b][:, 2 * k2:2 * k2 + 2, :].rearrange("p h e -> p (h e)"),
                                                             identity=c['ident'][:, :]), reads=[K_('ya'), 'ident'], writes=[K_('ptr')])
            copy_op(g, 'act', yaT[b][:, :, :], ptr[b][:, 0:2, :], reads=[K_('ptr')], writes=[K_('yaT')])
            for half in range(2):
                cs_ = slice(half * 512, (half + 1) * 512)
                gc_ = slice(1024 + half * 512, 1024 + (half + 1) * 512)
                pa, pak = pA[na % 3], 'c_pA%d' % (na % 3)
                na += 1
                pg, pgk = pG[ng % 3], 'c_pG%d' % (ng % 3)
                sgt, sgk = sg[ng % 3], 'c_sg%d' % (ng % 3)
                ng += 1
                for kc in range(2):
                    g.op('pe', lambda e, pa=pa, kc=kc, cs_=cs_, b=b: e.matmul(pa[:, :], lhsT=yaT[b][:, kc, :], rhs=Wpa[:, kc, cs_],
                                                                            start=(kc == 0), stop=(kc == 1)), reads=[K_('yaT')] + wk('Wpa', kc, half * 512, half * 512 + 512), writes=[pak])
                for kc in range(8):
                    g.op('pe', lambda e, pg=pg, kc=kc, gc_=gc_, b=b: e.matmul(pg[:, :], lhsT=xT[b][:, kc, :], rhs=Wg[:, kc, gc_],
                                                                            start=(kc == 0), stop=False), reads=[K_('xT')] + wk('Wg', kc, 1024 + half * 512, 1024 + half * 512 + 512), writes=[pgk])
                g.op('pe', lambda e, pg=pg, gc_=gc_: e.matmul(pg[:, :], lhsT=c['onesb'][0:1, 0:128], rhs=bgb[0:1, gc_], start=False, stop=True),
                     reads=['onesb', 'bgb'], writes=[pgk])
                g.op('act', lambda e, pg=pg, sgt=sgt: e.activation(out=sgt[:, :], in_=pg[:, :], func=AF.Sigmoid), reads=[pgk], writes=[sgk])
                g.op('dve', lambda e, pa=pa, sgt=sgt, b=b: e.tensor_tensor(out=mt2[b][:, :], in0=pa[:, :], in1=sgt[:, :], op=ALU.mult),
                     reads=[pak, sgk], writes=[K_('m2')])
                g.op('dve', lambda e, cs_=cs_, b=b: e.tensor_tensor(out=mb[b][:, cs_], in0=mm[b][:, cs_], in1=mt2[b][:, :], op=ALU.add),
                     reads=[K_('m'), K_('m2')], writes=[K_('mb')])
            for k8 in range(8):
                g.op('pe', lambda e, k8=k8, b=b: e.transpose(out=ptr[b][:, k8, :], in_=mb[b][:, k8 * 128:(k8 + 1) * 128], identity=c['ident'][:, :]),
                     reads=[K_('mb'), 'ident'], writes=[K_('ptr')])
            copy_op(g, 'act', mT[b][:, :, :], ptr[b][:, :, :], reads=[K_('ptr')], writes=[K_('mT')])
            for half in range(2):
                cs_ = slice(half * 512, (half + 1) * 512)
                pa, pak = pA[na % 3], 'c_pA%d' % (na % 3)
                na += 1
                for kc in range(8):
                    g.op('pe', lambda e, pa=pa, kc=kc, cs_=cs_, b=b: e.matmul(pa[:, :], lhsT=mT[b][:, kc, :], rhs=Wo[:, kc, cs_],
                                                                            start=(kc == 0), stop=(kc == 7)), reads=[K_('mT')] + wk('Wo', kc, half * 512, half * 512 + 512), writes=[pak])
                g.op('dve', lambda e, pa=pa, cs_=cs_, b=b: e.scalar_tensor_tensor(out=rr_[b][:, cs_], in0=xin[b][:, 0, cs_], scalar=ALPHA, in1=pa[:, :],
                                                                               op0=ALU.mult, op1=ALU.add), reads=[K_('xin'), pak], writes=[K_('r')])
            layer_norm_ops(g, rr_[b], K_('r'), lng, lnb, h1[b], K_('h1'), h1[b], K_('h1'), st[b], K_('st'))
            g.dma('pool', 'Sh1', scr['h1'][t0:t0 + 128, :], h1[b][:, :], reads=[K_('h1')], writes=[('scr_h1', ch)])
        g.drain('sp')
        g.flush()


def phase_D(nc, g, c, S, io, scr):
    T = 256
    NT = S // T
    with contextlib.ExitStack() as es:
        sb = lambda name, shape, dt: es.enter_context(nc.sbuf_tensor(name, shape, dt))
        ps = lambda name, shape, dt: es.enter_context(nc.psum_tensor(name, shape, dt))
        Wu = sb("Wu", [128, 8, DFF], BF16)
        Wd = sb("Wd", [128, 32, D], BF16)
        wst = [sb("wstd%d" % i, [128, 512], F32) for i in range(4)]
        lng = sb("lng2", [128, 1024], F32)
        lnb = sb("lnb2", [128, 1024], F32)
        hin = [sb("d_hin%d" % i, [128, 2, 1024], F32) for i in range(2)]
        hb = [sb("d_hb0", [128, 2, 1024], BF16)] * 2
        hT = [sb("d_hT%d" % i, [128, 8, T], BF16) for i in range(2)]
        f1 = sb("d_f1", [128, 32, T], BF16)
        rl = [sb("d_rl%d" % i, [128, 512], F32) for i in range(2)]
        r2 = [sb("d_r20", [128, 1024], F32)] * 2
        o2 = [sb("d_o0", [128, 1024], F32)] * 2
        st = [sb("d_st0", [128, 20], F32)] * 2
        ptr = [ps("d_ptr%d" % i, [128, 8, 128], BF16) for i in range(2)]
        pu = [ps("d_pu%d" % i, [128, 512], F32) for i in range(3)]
        pd = [ps("d_pd%d" % i, [128, 512], F32) for i in range(2)]
        rr = RR(['dve', 'act'])
        load_weight_bf16(g, nc, es, Wu, io['w_up'], 0, 0, DFF, 'Wu', 8, wst, ['wstd%d' % i for i in range(4)], rr, chunk=512)
        load_weight_bf16(g, nc, es, Wd, io['w_down'], 0, 0, D, 'Wd', 32, wst, ['wstd%d' % i for i in range(4)], rr, chunk=512)
        bcast_row(g, lng, 'lng', io['ln2_g'], 1024)
        bcast_row(g, lnb, 'lnb', io['ln2_b'], 1024)
        for i in range(NT):
            hs = i % 2
            x_tile_T(g, nc, c, scr['h1'], i * T, 2, hin[hs], hb[hs], hT[hs], ptr[hs], ('d_hin%d' % hs, 'd_hb0', 'd_hT%d' % hs, 'd_ptr%d' % hs), 'dve')
            for fp in range(16):
                pb, pk = pu[fp % 3], 'd_pu%d' % (fp % 3)
                for half in range(2):
                    ft = fp * 2 + half
                    for kc in range(8):
                        g.op('pe', lambda e, pb=pb, half=half, ft=ft, kc=kc, hs=hs: e.matmul(pb[:, half * T:(half + 1) * T], lhsT=Wu[:, kc, ft * 128:(ft + 1) * 128],
                                                                                rhs=hT[hs][:, kc, :], start=(kc == 0), stop=(kc == 7)),
                             reads=wk('Wu', kc, ft * 128, ft * 128 + 128) + ['d_hT%d' % hs], writes=[pk])
                rt, rk = rl[fp % 2], 'd_rl%d' % (fp % 2)
                g.op('act', lambda e, pb=pb, rt=rt: e.activation(out=rt[:, :], in_=pb[:, :], func=AF.Relu), reads=[pk], writes=[rk])
                g.op('dve', lambda e, rt=rt, fp=fp: e.tensor_tensor(
                    out=f1[:, fp * 2:fp * 2 + 2, :], in0=rt[:, :].rearrange("p (a t) -> p a t", a=2), in1=rt[:, :].rearrange("p (a t) -> p a t", a=2),
                    op=ALU.mult), reads=[rk], writes=['d_f1'])
            for j in range(2):
                for half in range(2):
                    cs_ = slice(half * 512, (half + 1) * 512)
                    for kc in range(32):
                        g.op('pe', lambda e, half=half, kc=kc, j=j, cs_=cs_: e.matmul(pd[half][:, :], lhsT=f1[:, kc, j * 128:(j + 1) * 128], rhs=Wd[:, kc, cs_],
                                                                                  start=(kc == 0), stop=(kc == 31)), reads=['d_f1'] + wk('Wd', kc, half * 512, half * 512 + 512), writes=['d_pd%d' % half])
                    g.op('dve', lambda e, half=half, j=j, cs_=cs_, hs=hs: e.scalar_tensor_tensor(out=r2[j][:, cs_], in0=hin[hs][:, j, cs_], scalar=ALPHA, in1=pd[half][:, :],
                                                                                       op0=ALU.mult, op1=ALU.add), reads=['d_hin%d' % hs, 'd_pd%d' % half], writes=['d_r20'])
                ob, obk = o2[j], 'd_o0'
                layer_norm_ops(g, r2[j], 'd_r20', lng, lnb, ob, obk, ob, obk, st[j], 'd_st0')
                t0 = i * T + j * 128
                g.dma('pool', 'Sout%d' % j, io['out'][t0:t0 + 128, :], ob[:, :], reads=[obk], writes=[('out', t0)])
        g.drain('sp')
        g.flush()


def build(S=SEQ, phases=('A1',), debug=False):
    nc = bass.Bass("TRN2", target_bir_lowering=False)
    io = {}
    specs = [("x", [S, D]), ("w_in", [D, INC]), ("b_gate", [2 * D]), ("conv_w", [5, CONV]), ("conv_b", [CONV]),
             ("dt_bias_f", [NH]), ("dt_bias_b", [NH]), ("a_log_f", [NH]), ("a_log_b", [NH]), ("d_skip", [NH]),
             ("ssd_norm_w", [DIN]), ("w_proj_ssd", [DIN, D]), ("w_proj_attn", [256, D]), ("w_out", [D, D]),
             ("ln1_g", [D]), ("ln1_b", [D]), ("w_up", [D, DFF]), ("w_down", [DFF, D]), ("ln2_g", [D]), ("ln2_b", [D])]
    for n, shp in specs:
        io[n] = nc.dram_tensor(n, shp, F32, kind="ExternalInput").ap()
    out = nc.dram_tensor("out", [S, D], F32, kind="ExternalOutput").ap()
    io['out'] = out
    kind = "ExternalOutput" if debug else "Internal"
    scr = {}
    for n, shp, dt in [("xc", [S, DIN], BF16), ("btok", [S, 512], BF16), ("bct", [1024, S], BF16), ("dts", [S, 128], F32),
                       ("qkT", [1536, S], BF16), ("v", [S, AW], BF16), ("yf", [S, DIN], BF16), ("yb", [S, DIN], BF16),
                       ("att0", [S, 288], F32), ("att1", [S, 288], F32), ("att2", [S, 288], F32), ("h1", [S, D], F32), ("yn", [S, DIN], BF16), ("xTs", [D, S], BF16), ("ynT", [DIN, S], BF16)]:
        scr[n] = nc.dram_tensor("scr_" + n, shp, dt, kind=kind).ap()
    with contextlib.ExitStack() as es:
        g = G(nc, es)
        c = make_consts(nc, g, es)
        if 'A1' in phases:
            phase_A1(nc, g, c, S, io, scr)
        if 'A2' in phases:
            phase_A2(nc, g, c, S, io, scr)
        if 'SF' in phases:
            phase_SSD(nc, g, c, S, io, scr, 0)
        if 'SB' in phases:
            phase_SSD(nc, g, c, S, io, scr, 1)
        if 'AT' in phases:
            phase_AT(nc, g, c, S, io, scr)
        if 'C' in phases or 'CZ' in phases:
            phase_CZ(nc, g, c, S, io, scr)
        if 'C' in phases or 'CC' in phases:
            phase_C(nc, g, c, S, io, scr)
        if 'D' in phases:
            phase_D(nc, g, c, S, io, scr)
        g.drain('sp')
        g.flush()
        print("instructions:", g.nins)
    return nc


ALL_PHASES = ('A1', 'A2', 'SF', 'SB', 'AT', 'C', 'D')


def kernel(**inputs):
    x = np.asarray(inputs['x'], dtype=np.float32)
    nb = x.shape[0]
    nc = build(S=SEQ, phases=ALL_PHASES, debug=False)
    shared = {k: np.ascontiguousarray(np.asarray(v, dtype=np.float32)) for k, v in inputs.items() if k != 'x'}
    in_maps = []
    for b in range(nb):
        m = dict(shared)
        m['x'] = np.ascontiguousarray(x[b])
        in_maps.append(m)
    res = run_bass_kernel_spmd(nc, in_maps, core_ids=list(range(nb)))
    return np.stack([np.asarray(r['out'], dtype=np.float32) for r in res.results], axis=0)
```
